# Optimizing a Trainium2 kernel written in Bass

```python
import math
import jax, jax.numpy as jnp
from jax import lax
import numpy as np

D_MODEL = 4096
BATCH = 4
SEQ = 2048
DEPTH = 2
DEC_BATCH = 8
DEC_SEQ = 2048
PAST_LEN = 128

N_MEM = 256
SGU_CHUNK = 128
SGU_GROUPS = 12
SGU_GROUP_DIM = 128
W_A = SGU_GROUPS * SGU_GROUP_DIM
GDN_HEADS = 12
GDN_HEAD_DIM = 128
W_B = GDN_HEADS * GDN_HEAD_DIM
GDN_CHUNK = 64
CONV_K = 5
XA_HEADS = 4
XA_HEAD_DIM = 256
W_C = XA_HEADS * XA_HEAD_DIM
W_MIX = W_A + W_B + W_C
EPS = 1e-6
IN_SIZES = (W_A, W_A, W_A, W_B, W_B, W_B, W_B, GDN_HEADS, GDN_HEADS, GDN_HEADS, GDN_HEADS, W_C, W_C)
N_IN = sum(IN_SIZES)

kernel_name = 'hybrid_sgu_gdn_memxattn_encoder'


def rms_norm(x, g):
    xf = x.astype(jnp.float32)
    y = xf * lax.rsqrt(jnp.mean(xf * xf, axis=-1, keepdims=True) + EPS)
    return (y * g.astype(jnp.float32)).astype(x.dtype)


def layer_norm(x, g, b):
    xf = x.astype(jnp.float32)
    mu = jnp.mean(xf, axis=-1, keepdims=True)
    var = jnp.mean(jnp.square(xf - mu), axis=-1, keepdims=True)
    return ((xf - mu) * lax.rsqrt(var + EPS) * g.astype(jnp.float32) + b.astype(jnp.float32)).astype(x.dtype)


def l2_normalize(x):
    return x * lax.rsqrt(jnp.sum(x * x, axis=-1, keepdims=True) + EPS)


def spatial_gating(u, v, ln_g, ln_b, w_s, b_s):
    B, S, _ = u.shape
    n = S // SGU_CHUNK
    vn = layer_norm(v, ln_g, ln_b).reshape(B, n, SGU_CHUNK, SGU_GROUPS, SGU_GROUP_DIM)
    mixed = jnp.einsum('gts,bnsgc->bntgc', w_s, vn) + b_s.T[None, None, :, :, None]
    return u * mixed.reshape(B, S, W_A)


def centred_depthwise_conv(x, w):
    K, C = w.shape
    return lax.conv_general_dilated(
        x, w[:, None, :].astype(x.dtype), window_strides=(1,),
        padding=[((K - 1) // 2, K // 2)],
        dimension_numbers=('NWC', 'WIO', 'NWC'), feature_group_count=C)


def gated_delta_chunked(q, k, v, g, beta):
    B, S, H, Dk = q.shape
    Dv = v.shape[-1]
    C = GDN_CHUNK
    n = S // C

    def to_chunks(t):
        return jnp.moveaxis(t.reshape(B, n, C, H, -1), 3, 1)

    qc, kc, vc = to_chunks(q), to_chunks(k), to_chunks(v)
    gc = jnp.cumsum(jnp.moveaxis(g.reshape(B, n, C, H), 3, 1), axis=-1)
    bc = jnp.moveaxis(beta.reshape(B, n, C, H), 3, 1)
    lower = jnp.tril(jnp.ones((C, C), dtype=bool))
    strict = jnp.tril(jnp.ones((C, C), dtype=bool), -1)
    diff = gc[..., :, None] - gc[..., None, :]
    decay = jnp.where(lower, jnp.exp(jnp.where(lower, diff, 0.0)), 0.0)
    k_beta = kc * bc[..., None]
    v_beta = vc * bc[..., None]
    a = jnp.where(strict, jnp.einsum('bhnid,bhnjd->bhnij', k_beta, kc) * decay, 0.0)
    eye = jnp.eye(C, dtype=a.dtype)
    t_mat = lax.linalg.triangular_solve(a + eye, jnp.broadcast_to(eye, a.shape), left_side=True, lower=True)
    u = t_mat @ v_beta
    w = t_mat @ (k_beta * jnp.exp(gc)[..., None])
    qk = jnp.einsum('bhnid,bhnjd->bhnij', qc, kc) * decay

    def step(state, xs):
        q_i, k_i, u_i, w_i, g_i, qk_i = xs
        v_new = u_i - w_i @ state
        out = (q_i * jnp.exp(g_i)[..., None]) @ state + qk_i @ v_new
        g_last = g_i[..., -1:]
        state = state * jnp.exp(g_last)[..., None] + jnp.einsum(
            'bhcd,bhce->bhde', k_i * jnp.exp(g_last - g_i)[..., None], v_new)
        return state, out

    xs = tuple(jnp.moveaxis(t, 2, 0) for t in (qc, kc, u, w, gc, qk))
    state0 = jnp.zeros((B, H, Dk, Dv), jnp.float32)
    _, out = lax.scan(step, state0, xs)
    return jnp.transpose(out, (1, 0, 3, 2, 4)).reshape(B, S, H, Dv)


def memory_attention(q, mem, mem_norm_g, w_mem_kv):
    B, S, _ = q.shape
    m = rms_norm(mem, mem_norm_g)
    k, v = jnp.split(m @ w_mem_kv, 2, axis=-1)
    q = q.reshape(B, S, XA_HEADS, XA_HEAD_DIM)
    k = k.reshape(B, -1, XA_HEADS, XA_HEAD_DIM)
    v = v.reshape(B, -1, XA_HEADS, XA_HEAD_DIM)
    s = jnp.einsum('bshd,bmhd->bhsm', q, k).astype(jnp.float32) * (XA_HEAD_DIM ** -0.5)
    p = jax.nn.softmax(s, axis=-1).astype(v.dtype)
    return jnp.einsum('bhsm,bmhd->bshd', p, v).reshape(B, S, W_C)


def encoder_layer(x, mem, norm_g, w_in, sgu_ln_g, sgu_ln_b, sgu_w, sgu_b, conv_w,
                  a_log, dt_bias, gdn_norm_g, mem_norm_g, w_mem_kv, w_out):
    B, S, _ = x.shape
    h = rms_norm(x, norm_g)
    z = h @ w_in
    split_at = np.cumsum(IN_SIZES)[:-1].tolist()
    (u_a, v_a, gate_a, q_b, k_b, v_b, gate_b, beta_fw, beta_bw, dec_fw, dec_bw,
     q_c, gate_c) = jnp.split(z, split_at, axis=-1)

    y_a = spatial_gating(jax.nn.gelu(u_a), jax.nn.gelu(v_a), sgu_ln_g, sgu_ln_b, sgu_w, sgu_b) * jax.nn.silu(gate_a)

    qkv = jax.nn.silu(centred_depthwise_conv(jnp.concatenate([q_b, k_b, v_b], axis=-1), conv_w))
    qkv = qkv.astype(jnp.float32).reshape(B, S, 3, GDN_HEADS, GDN_HEAD_DIM)
    q = l2_normalize(qkv[:, :, 0]) * (GDN_HEAD_DIM ** -0.5)
    k = l2_normalize(qkv[:, :, 1])
    v = qkv[:, :, 2]
    a_log32 = a_log.astype(jnp.float32)
    dt32 = dt_bias.astype(jnp.float32)
    g_fw = -jnp.exp(a_log32[0]) * jax.nn.softplus(dec_fw.astype(jnp.float32) + dt32[0])
    g_bw = -jnp.exp(a_log32[1]) * jax.nn.softplus(dec_bw.astype(jnp.float32) + dt32[1])
    b_fw = jax.nn.sigmoid(beta_fw.astype(jnp.float32))
    b_bw = jax.nn.sigmoid(beta_bw.astype(jnp.float32))
    o_fw = gated_delta_chunked(q, k, v, g_fw, b_fw)
    flip = lambda t: jnp.flip(t, axis=1)
    o_bw = flip(gated_delta_chunked(flip(q), flip(k), flip(v), flip(g_bw), flip(b_bw)))
    o_b = rms_norm(o_fw + o_bw, gdn_norm_g)
    y_b = o_b.reshape(B, S, W_B).astype(x.dtype) * jax.nn.silu(gate_b)

    y_c = memory_attention(q_c, mem, mem_norm_g, w_mem_kv) * jax.nn.silu(gate_c)

    y = jnp.concatenate([y_a, y_b, y_c.astype(x.dtype)], axis=-1) @ w_out
    return x + y


def encoder_trunk(x, mem, norm_g, w_in, sgu_ln_g, sgu_ln_b, sgu_w, sgu_b, conv_w,
                  a_log, dt_bias, gdn_norm_g, mem_norm_g, w_mem_kv, w_out, final_g):
    for l in range(DEPTH):
        x = encoder_layer(x, mem, norm_g[l], w_in[l], sgu_ln_g[l], sgu_ln_b[l], sgu_w[l], sgu_b[l],
                          conv_w[l], a_log[l], dt_bias[l], gdn_norm_g[l], mem_norm_g[l],
                          w_mem_kv[l], w_out[l])
    return rms_norm(x, final_g)


def setup_inputs(seed: int = 0) -> dict:
    key = jax.random.key(seed)
    ks = jax.random.split(key, 20)
    nrm = jax.random.normal
    f32 = jnp.float32
    dt = jnp.exp(jax.random.uniform(ks[11], (DEPTH, 2, GDN_HEADS), f32, math.log(1e-3), math.log(1e-1)))
    return {
        'x_prompt': nrm(ks[0], (BATCH, SEQ, D_MODEL), f32),
        'x_sample': nrm(ks[1], (DEC_BATCH, DEC_SEQ, D_MODEL), f32),
        'mem_prompt': nrm(ks[2], (BATCH, N_MEM, D_MODEL), f32),
        'mem_sample': nrm(ks[3], (DEC_BATCH, N_MEM, D_MODEL), f32),
        'norm_g': 1.0 + 0.02 * nrm(ks[4], (DEPTH, D_MODEL), f32),
        'w_in': nrm(ks[5], (DEPTH, D_MODEL, N_IN), f32) * D_MODEL ** -0.5,
        'sgu_ln_g': 1.0 + 0.02 * nrm(ks[6], (DEPTH, W_A), f32),
        'sgu_ln_b': 0.02 * nrm(ks[7], (DEPTH, W_A), f32),
        'sgu_w': nrm(ks[8], (DEPTH, SGU_GROUPS, SGU_CHUNK, SGU_CHUNK), f32) * (0.5 * SGU_CHUNK ** -0.5),
        'sgu_b': 1.0 + 0.02 * nrm(ks[9], (DEPTH, SGU_GROUPS, SGU_CHUNK), f32),
        'conv_w': nrm(ks[10], (DEPTH, CONV_K, 3 * W_B), f32) * CONV_K ** -0.5,
        'a_log': jnp.log(jax.random.uniform(ks[12], (DEPTH, 2, GDN_HEADS), f32, 1.0, 16.0)),
        'dt_bias': dt + jnp.log(-jnp.expm1(-dt)),
        'gdn_norm_g': 1.0 + 0.02 * nrm(ks[13], (DEPTH, GDN_HEAD_DIM), f32),
        'mem_norm_g': 1.0 + 0.02 * nrm(ks[14], (DEPTH, D_MODEL), f32),
        'w_mem_kv': nrm(ks[15], (DEPTH, D_MODEL, 2 * W_C), f32) * D_MODEL ** -0.5,
        'w_out': nrm(ks[16], (DEPTH, W_MIX, D_MODEL), f32) * W_MIX ** -0.5,
        'final_g': 1.0 + 0.02 * nrm(ks[17], (D_MODEL,), f32),
    }


def reference(x_prompt, x_sample, mem_prompt, mem_sample, norm_g, w_in, sgu_ln_g, sgu_ln_b,
              sgu_w, sgu_b, conv_w, a_log, dt_bias, gdn_norm_g, mem_norm_g, w_mem_kv, w_out, final_g):
    y_prompt = encoder_trunk(x_prompt, mem_prompt, norm_g, w_in, sgu_ln_g, sgu_ln_b, sgu_w, sgu_b,
                             conv_w, a_log, dt_bias, gdn_norm_g, mem_norm_g, w_mem_kv, w_out, final_g)
    y_sample = encoder_trunk(x_sample, mem_sample, norm_g, w_in, sgu_ln_g, sgu_ln_b, sgu_w, sgu_b,
                             conv_w, a_log, dt_bias, gdn_norm_g, mem_norm_g, w_mem_kv, w_out, final_g)
    return (y_prompt, y_sample)
```

```python
import numpy as np
import concourse.bass as bass
import concourse.mybir as mybir
from concourse.bass_utils import run_bass_kernel_spmd

F32 = mybir.dt.float32
BF16 = mybir.dt.bfloat16
F32R = mybir.dt.float32r
AF = mybir.ActivationFunctionType
ALU = mybir.AluOpType
AX = mybir.AxisListType

D_MODEL = 4096
SEQ = 2048
NT = SEQ // 128
N_IN = 12848
EPS = 1e-6
BIGM = 30000.0


class Res:
    __slots__ = ("w", "r")

    def __init__(self):
        self.w = None
        self.r = []


class Sched:
    ENGS = ("pe", "act", "dve", "pool", "sp")
    NDMA = {"sp": 16, "act": 6, "pool": 8}

    def __init__(self, nc):
        self.nc = nc
        self.ops = {e: [] for e in self.ENGS}
        self.sem = {e: nc.alloc_semaphore(name="s_" + e) for e in self.ENGS}
        self.cnt = {e: 0 for e in self.ENGS}
        self.waited = {e: {} for e in self.ENGS}
        self.dsem = {q: [nc.alloc_semaphore(name="d_%s%d" % (q, i)) for i in range(n)] for q, n in self.NDMA.items()}
        self.dcnt = {q: [0] * n for q, n in self.NDMA.items()}
        self.drr = {q: 0 for q in self.NDMA}
        self.res = {}
        self.n_ops = 0

    def R(self, key):
        r = self.res.get(key)
        if r is None:
            r = self.res[key] = Res()
        return r

    @staticmethod
    def _flat(xs):
        out = []
        for x in xs:
            if isinstance(x, (list, tuple)):
                out.extend(x)
            else:
                out.append(x)
        return out

    def _deps(self, reads, writes):
        deps = []
        for r in reads:
            if r.w is not None:
                deps.append(r.w)
        for w in writes:
            if w.w is not None:
                deps.append(w.w)
            deps.extend(w.r)
        return deps

    def _waits(self, eng, deps):
        out = {}
        wd = self.waited[eng]
        for (k, v) in deps:
            if k == "pe" and eng == "pe":
                continue
            if k == eng and v > self.cnt[eng]:
                continue
            if wd.get(k, 0) >= v:
                continue
            if out.get(k, 0) < v:
                out[k] = v
        for k, v in out.items():
            wd[k] = v
        return list(out.items())

    def op(self, eng, fn, reads=(), writes=(), inc=True, defer=False):
        self.n_ops += 1
        reads = self._flat(reads)
        writes = self._flat(writes)
        deps = self._deps(reads, writes)
        waits = self._waits(eng, deps)
        if not inc:
            self.ops[eng].append((waits, fn, None))
            return None
        if defer:
            ev = (eng, self.cnt[eng] + 1)
            self.ops[eng].append((waits, fn, None))
        else:
            self.cnt[eng] += 1
            ev = (eng, self.cnt[eng])
            self.ops[eng].append((waits, fn, ev))
        for w in writes:
            w.w = ev
            w.r = []
        for r in reads:
            r.r.append(ev)
        return ev

    def dma(self, q, fn, reads=(), writes=()):
        self.n_ops += 1
        reads = self._flat(reads)
        writes = self._flat(writes)
        n = self.NDMA[q]
        i = self.drr[q] % n
        self.drr[q] += 1
        s = self.dsem[q][i]
        prev = self.dcnt[q][i]
        deps = self._deps(reads, writes)
        if prev > 0:
            deps.append((s, prev))
        waits = self._waits(q, deps)
        self.dcnt[q][i] = prev + 16
        ev = (s, prev + 16)
        self.ops[q].append((waits, fn, ev))
        for w in writes:
            w.w = ev
            w.r = []
        for r in reads:
            r.r.append(ev)
        return ev

    def all_events(self):
        evs = []
        for e in self.ENGS:
            if self.cnt[e] > 0:
                evs.append((e, self.cnt[e]))
        for q in self.NDMA:
            for s, c in zip(self.dsem[q], self.dcnt[q]):
                if c > 0:
                    evs.append((s, c))
        return evs

    def barrier(self):
        evs = self.all_events()
        for e in self.ENGS:
            waits = self._waits(e, evs)
            if waits:
                self.ops[e].append((waits, None, None))
        for r in self.res.values():
            r.w = None
            r.r = []

    def emit(self, block):
        self.barrier()
        needed = {e: set() for e in self.ENGS}
        for e in self.ENGS:
            for waits, fn, own in self.ops[e]:
                for k, v in waits:
                    if isinstance(k, str):
                        needed[k].add(v)
        cmap = {}
        for e in self.ENGS:
            cmap[e] = {v: i + 1 for i, v in enumerate(sorted(needed[e]))}
        self.n_incs = {e: len(needed[e]) for e in self.ENGS}

        def run(e, ename):
            for waits, fn, own in self.ops[ename]:
                for k, v in waits:
                    if isinstance(k, str):
                        e.wait_ge(self.sem[k], cmap[k][v])
                    else:
                        e.wait_ge(k, v)
                if fn is None:
                    continue
                ins = fn(e)
                if own is not None:
                    if isinstance(own[0], str):
                        if own[1] in needed[ename]:
                            ins.then_inc(self.sem[ename], 1)
                    else:
                        ins.then_inc(own[0], own[1] if False else 16)

        block.tensor(lambda e: run(e, "pe"))
        block.scalar(lambda e: run(e, "act"))
        block.vector(lambda e: run(e, "dve"))
        block.gpsimd(lambda e: run(e, "pool"))
        block.sync(lambda e: run(e, "sp"))


class Mem:
    def __init__(self, nc, limit=229000):
        self.nc = nc
        self.top = 16640
        self.n = 0
        self.limit = limit
        self.peak = 0

    def alloc(self, shape, dtype):
        nb = 2 if dtype == BF16 else 4
        n = 1
        for s in shape[1:]:
            n *= s
        off = (self.top + 63) // 64 * 64
        self.top = off + n * nb
        self.peak = max(self.peak, self.top)
        assert self.top <= self.limit, ("SBUF overflow", self.top)
        self.n += 1
        return self.nc.alloc_sbuf_tensor_at("t%d" % self.n, list(shape), dtype, offset=off)

    def mark(self):
        return self.top

    def release(self, m):
        self.top = m


def build_program(NSEQ=2, debug=False, layers=(0, 1), phases=("P1", "P2", "KV", "A", "B", "C", "OUT", "FIN"), heads=tuple(range(12))):
    nc = bass.Bass("TRN2", target_bir_lowering=False)

    def din(name, shape):
        return nc.dram_tensor(name, list(shape), F32, kind="ExternalInput").ap()

    x_in = din("x", [NSEQ, SEQ, D_MODEL])
    mem_in = din("mem", [NSEQ, 256, D_MODEL])
    norm_g = din("norm_g", [2, D_MODEL])
    big = any(p in phases for p in ("P2", "OUT", "KV"))
    w_in = din("w_in", [2, D_MODEL, N_IN] if big else [2, 8, 8])
    sgu_ln_g = din("sgu_ln_g", [2, 1536])
    sgu_ln_b = din("sgu_ln_b", [2, 1536])
    sgu_w = din("sgu_w", [2, 12, 128, 128])
    sgu_b = din("sgu_b", [2, 12, 128])
    conv_w = din("conv_w", [2, 5, 4608])
    a_log = din("a_log", [2, 2, 12])
    dt_bias = din("dt_bias", [2, 2, 12])
    gdn_norm_g = din("gdn_norm_g", [2, 128])
    mem_norm_g = din("mem_norm_g", [2, D_MODEL])
    w_mem_kv = din("w_mem_kv", [2, D_MODEL, 2048] if big else [2, 8, 8])
    w_out = din("w_out", [2, D_MODEL, D_MODEL] if big else [2, 8, 8])
    final_g = din("final_g", [D_MODEL])
    out = nc.dram_tensor("out", [NSEQ, SEQ, D_MODEL], F32, kind="ExternalOutput").ap()
    skind = "ExternalOutput" if debug else "Internal"
    zT = nc.dram_tensor("zT", [N_IN, SEQ], F32, kind=skind).ap()
    va_tok = nc.dram_tensor("va_tok", [SEQ, 1536], F32, kind=skind).ap()
    yT = nc.dram_tensor("yT", [D_MODEL, SEQ], BF16, kind=skind).ap()
    x1 = nc.dram_tensor("x1", [NSEQ, SEQ, D_MODEL], F32, kind=skind).ap()

    S = Sched(nc)
    M = Mem(nc)
    R = S.R
    PS = [nc.alloc_psum_tensor("psb%d" % i, [128, 512], F32) for i in range(8)]
    RP = [[R("ps%da" % i), R("ps%db" % i)] for i in range(8)]

    def RH(i, half):
        return RP[i]

    def ph(i, half):
        return PS[i][:, half * 256:(half + 1) * 256].rearrange("p (a b) -> p a b", b=128)

    def phb(i, half):
        return PS[i][:, half * 256:(half + 1) * 256].bitcast(BF16).rearrange("p (a b) -> p a b", b=128)

    dbg_seen = set()

    def dbg(name, t, shape, dtype, res):
        if not debug or ("dbg_" + name) in dbg_seen:
            return
        dbg_seen.add("dbg_" + name)
        dt_ = nc.dram_tensor("dbg_" + name, list(shape), dtype, kind="ExternalOutput").ap()
        S.dma("sp", lambda e: e.dma_start(out=dt_, in_=t[:]), reads=[res], writes=[R("dbg_" + name)])

    def psb(i):
        return PS[i][:].bitcast(BF16).rearrange("p (a b) -> p a b", b=128)

    def psf(i):
        return PS[i][:].rearrange("p (a b) -> p a b", b=128)

    ident_f = M.alloc([128, 128], F32)
    ident_b = M.alloc([128, 128], BF16)
    ones_f = M.alloc([128, 128], F32)
    ones_b = M.alloc([128, 128], BF16)
    Ltri = M.alloc([128, 128], F32)
    Utri = M.alloc([128, 128], F32)
    BIG = [M.alloc([128, 4, 128], F32) for _ in range(2)]
    ST01 = [M.alloc([128, 128], F32) for _ in range(2)]
    eps_t = M.alloc([128, 1], F32)
    one_t = M.alloc([128, 1], F32)
    RC = R("consts")

    def sel_fill(t_ap, val, pattern, cm, cmp):
        S.op("pool", lambda e: e.memset(t_ap, val), writes=[RC])
        S.op("pool", lambda e: e.affine_select(out=t_ap, in_=t_ap, pattern=pattern, compare_op=cmp, fill=0.0, base=0, channel_multiplier=cm), reads=[RC], writes=[RC])

    sel_fill(ident_f[:], 1.0, [[-1, 128]], 1, ALU.is_equal)
    S.op("pool", lambda e: e.memset(ones_f[:], 1.0), writes=[RC])
    S.op("pool", lambda e: e.memset(ones_b[:], 1.0), writes=[RC])
    S.op("pool", lambda e: e.memset(eps_t[:], EPS), writes=[RC])
    S.op("pool", lambda e: e.memset(one_t[:], 1.0), writes=[RC])
    S.op("pool", lambda e: e.tensor_copy(out=ident_b[:], in_=ident_f[:]), reads=[RC], writes=[RC])
    sel_fill(Ltri[:], 1.0, [[1, 128]], -1, ALU.is_ge)
    sel_fill(Utri[:], 1.0, [[-1, 128]], 1, ALU.is_ge)
    sel_fill(BIG[0][:], BIGM, [[0, 4], [1, 128]], -1, ALU.is_gt)
    sel_fill(BIG[1][:], BIGM, [[0, 4], [-1, 128]], 1, ALU.is_gt)
    sel_fill(ST01[0][:], 1.0, [[-1, 128]], 1, ALU.is_gt)
    sel_fill(ST01[1][:], 1.0, [[1, 128]], -1, ALU.is_gt)

    ng_col = M.alloc([128, 32], F32)
    mg_col = M.alloc([128, 32], F32)
    lng_col = M.alloc([128, 12], F32)
    lnb_col = M.alloc([128, 12], F32)
    cw_col = M.alloc([128, 36, 5], F32)
    gn_col = M.alloc([128, 1], F32)
    wsT = M.alloc([128, 12, 128], BF16)
    Rsg = M.alloc([128, 12, 128], F32)
    negA = M.alloc([128, 24], F32)
    dtb = M.alloc([128, 24], F32)
    bd_tok = M.alloc([128, NT, 48], F32)
    RPAR = R("params")
    RTAB = R("tables")
    base_mark = M.mark()

    def load_layer_params(l):
        mk = M.mark()
        st = M.alloc([36, 128], F32)
        st2 = M.alloc([128, 12, 128], F32)
        st3 = M.alloc([5, 4608], F32)
        bbc = M.alloc([128, 12, 128], F32)
        alb = M.alloc([128, 24], F32)
        Rst = R("pstage")

        def rows_to_cols(src_rows_ap, nrows, dst_ap):
            S.dma("sp", lambda e: e.dma_start(out=st[0:nrows, :], in_=src_rows_ap), writes=[Rst])
            S.op("pe", lambda e: e.matmul(PS[0][:, 0:nrows], st[0:nrows, :], ident_f[0:nrows, 0:nrows], start=True, stop=True), reads=[Rst, RC], writes=[RP[0]])
            S.op("dve", lambda e: e.tensor_copy(out=dst_ap, in_=PS[0][:, 0:nrows]), reads=[RP[0]], writes=[RPAR])

        rows_to_cols(norm_g[l].rearrange("(k p) -> k p", p=128), 32, ng_col[:])
        rows_to_cols(mem_norm_g[l].rearrange("(k p) -> k p", p=128), 32, mg_col[:])
        rows_to_cols(sgu_ln_g[l].rearrange("(k p) -> k p", p=128), 12, lng_col[:])
        rows_to_cols(sgu_ln_b[l].rearrange("(k p) -> k p", p=128), 12, lnb_col[:])
        rows_to_cols(gdn_norm_g[l].rearrange("(k p) -> k p", p=128), 1, gn_col[:])
        S.dma("sp", lambda e: e.dma_start(out=st3[:], in_=conv_w[l]), writes=[Rst])
        for t in range(36):
            S.op("pe", lambda e, t=t: e.matmul(PS[1][:, t * 5:(t + 1) * 5], st3[0:5, t * 128:(t + 1) * 128], ident_f[0:5, 0:5], start=True, stop=True),
                 reads=[Rst, RC], writes=[RP[1]], inc=(t == 35))
        S.op("dve", lambda e: e.tensor_copy(out=cw_col[:].rearrange("p a b -> p (a b)"), in_=PS[1][:, 0:180]), reads=[RP[1]], writes=[RPAR])
        S.dma("sp", lambda e: e.dma_start(out=st2[:], in_=sgu_w[l].rearrange("g t s -> t g s")), writes=[Rst])
        for g in range(12):
            S.op("pe", lambda e, g=g: e.matmul(PS[2 + g // 4][:, (g % 4) * 128:(g % 4 + 1) * 128], st2[:, g, :], ident_f[:], start=True, stop=True),
                 reads=[Rst, RC], writes=[RP[2 + g // 4]])
        for b in range(3):
            S.op("dve", lambda e, b=b: e.tensor_copy(out=wsT[:, b * 4:(b + 1) * 4, :], in_=psf(2 + b)), reads=[RP[2 + b]], writes=[RPAR])
        S.dma("sp", lambda e: e.dma_start(out=bbc[:].rearrange("p a b -> p (a b)"), in_=sgu_b[l].rearrange("g t -> (g t)").partition_broadcast(128)), writes=[Rst])
        for g in range(12):
            S.op("pe", lambda e, g=g: e.matmul(PS[5 + g // 4][:, (g % 4) * 128:(g % 4 + 1) * 128], ones_b[:], wsT[:, g, :], start=True, stop=True),
                 reads=[RPAR, RC], writes=[RP[5 + g // 4]])
        for g in range(12):
            S.op("dve", lambda e, g=g: e.scalar_tensor_tensor(out=Rsg[:, g, :], in0=PS[5 + g // 4][:, (g % 4) * 128:(g % 4 + 1) * 128], scalar=lnb_col[:, g:g + 1], in1=bbc[:, g, :], op0=ALU.mult, op1=ALU.add),
                 reads=[RP[5 + g // 4], RPAR, Rst], writes=[RPAR])
        S.dma("sp", lambda e: e.dma_start(out=alb[:], in_=a_log[l].rearrange("d h -> (d h)").partition_broadcast(128)), writes=[Rst])
        S.dma("sp", lambda e: e.dma_start(out=dtb[:], in_=dt_bias[l].rearrange("d h -> (d h)").partition_broadcast(128)), writes=[RPAR])
        S.op("act", lambda e: e.activation(out=negA[:], in_=alb[:], func=AF.Exp), reads=[Rst], writes=[RPAR])
        S.op("dve", lambda e: e.tensor_scalar(out=negA[:], in0=negA[:], scalar1=-1.0, scalar2=0.0, op0=ALU.mult, op1=ALU.add), reads=[RPAR], writes=[RPAR])
        S.barrier()
        M.release(mk)

    def norm_transpose(src_fn, ntiles, gcol, dstT, tok0, rdst):
        mk = M.mark()
        xb = [M.alloc([128, D_MODEL], F32) for _ in range(2)]
        xs1 = M.alloc([128, D_MODEL], BF16)
        xs = [xs1, xs1]
        ss = [M.alloc([128, 1], F32) for _ in range(2)]
        rs = [M.alloc([128, 1], F32) for _ in range(2)]
        for t in range(ntiles):
            i = t % 2
            rx, rxs, rss, rrs = R("nx%d" % i), R("nxs"), R("nss%d" % i), R("nrs%d" % i)
            S.dma("sp", lambda e, t=t, i=i: e.dma_start(out=xb[i][:], in_=src_fn(t)), writes=[rx])
            S.op("pool", lambda e, i=i: e.memset(ss[i][:], 0.0), writes=[rss])
            S.op("act", lambda e, i=i: e.activation(out=xs[i][:], in_=xb[i][:], func=AF.Square, accum_out=ss[i][:]), reads=[rx], writes=[rxs, rss])
            S.op("act", lambda e, i=i: e.activation(out=rs[i][:], in_=ss[i][:], func=AF.Sqrt, bias=eps_t[:], scale=1.0 / D_MODEL), reads=[rss, RC], writes=[rrs])
            S.op("dve", lambda e, i=i: e.reciprocal(out=rs[i][:], in_=rs[i][:]), reads=[rrs], writes=[rrs])
            S.op("act", lambda e, i=i: e.activation(out=xs[i][:], in_=xb[i][:], func=AF.Identity, scale=rs[i][:]), reads=[rx, rrs], writes=[rxs])
            for kb in range(4):
                b = kb % 2
                for kk in range(8):
                    k = kb * 8 + kk
                    S.op("pe", lambda e, i=i, k=k, kk=kk, b=b: e.transpose(psb(b)[:, kk, :], xs[i][:, k * 128:(k + 1) * 128], ident_b[:]),
                         reads=[rxs, RC], writes=[RP[b]], inc=(kk == 7))
                S.op("dve", lambda e, kb=kb, b=b, t=t: e.tensor_tensor(out=dstT[:, kb * 8:(kb + 1) * 8, tok0 + t * 128:tok0 + (t + 1) * 128], in0=psb(b),
                                                                      in1=gcol[:, kb * 8:(kb + 1) * 8].unsqueeze(2).to_broadcast([128, 8, 128]), op=ALU.mult),
                     reads=[RP[b], RPAR], writes=[rdst])
        S.barrier()
        M.release(mk)

    wbufs = None

    def load_wblock(wsrc, c0, ncols, i):
        v = wsrc.rearrange("(k p) c -> p k c", p=128)
        dst = wbufs[i]
        for hh in range(2):
            S.dma("pool", lambda e, hh=hh, dst=dst: e.dma_start(out=dst[:, hh * 16:(hh + 1) * 16, 0:ncols], in_=v[:, hh * 16:(hh + 1) * 16, c0:c0 + ncols]), writes=[R("wb%d" % i)])

    def in_proj(l, xsrc):
        nonlocal wbufs
        mk = M.mark()
        hT = M.alloc([128, 32, 1024], BF16)
        wbufs = [M.alloc([128, 32, 256], BF16) for _ in range(2)]
        zst = [M.alloc([128, 1024], F32) for _ in range(2)]
        vst = [M.alloc([128, 256], F32) for _ in range(2)]
        rh = R("hT")
        blocks = []
        for c0 in range(0, 1536, 256):
            blocks.append(("fm", c0, 256))
        for c0 in range(1536, 3072, 256):
            blocks.append(("tm", c0, 256))
        for c0 in range(3072, 10752, 256):
            blocks.append(("fm", c0, 256))
        blocks.append(("bd", 10752, 48))
        for c0 in range(10800, N_IN, 256):
            blocks.append(("fm", c0, 256))
        for tb in range(2):
            norm_transpose(lambda t, tb=tb: xsrc[tb * 1024 + t * 128: tb * 1024 + (t + 1) * 128, :], 8, ng_col, hT, 0, rh)
            zi = 0
            vi = 0
            for bi, (kind, c0, ncols) in enumerate(blocks):
                wi = bi % 2
                rw = R("wb%d" % wi)
                load_wblock(w_in[l], c0, ncols, wi)
                if kind == "fm":
                    for c in range(2):
                        pb = (zi % 2) * 2
                        for k in range(32):
                            for hf in range(2):
                                S.op("pe", lambda e, wb=wbufs[wi], c=c, k=k, hf=hf, pb=pb: e.matmul(PS[pb + hf][:], wb[:, k, c * 128:(c + 1) * 128], hT[:, k, hf * 512:(hf + 1) * 512], start=(k == 0), stop=(k == 31)),
                                     reads=[rw, rh], writes=[RP[pb + hf]], inc=(k == 31))
                        zs = zi % 2
                        rz = R("zst%d" % zs)
                        S.op("act", lambda e, zs=zs, pb=pb: e.activation(out=zst[zs][:, 0:512], in_=PS[pb][:], func=AF.Copy), reads=[RP[pb]], writes=[rz])
                        S.op("dve", lambda e, zs=zs, pb=pb: e.tensor_copy(out=zst[zs][:, 512:1024], in_=PS[pb + 1][:]), reads=[RP[pb + 1]], writes=[rz])
                        r0 = c0 + c * 128
                        S.dma("sp", lambda e, zs=zs, r0=r0, tb=tb: e.dma_start(out=zT[r0:r0 + 128, tb * 1024:(tb + 1) * 1024], in_=zst[zs][:]), reads=[rz], writes=[R("zT")])
                        zi += 1
                elif kind == "tm":
                    for tt in range(8):
                        pb = 4 + (vi % 2)
                        for k in range(32):
                            S.op("pe", lambda e, wb=wbufs[wi], k=k, tt=tt, pb=pb: e.matmul(PS[pb][:, 0:256], hT[:, k, tt * 128:(tt + 1) * 128], wb[:, k, 0:256], start=(k == 0), stop=(k == 31)),
                                 reads=[rw, rh], writes=[RP[pb]], inc=(k == 31))
                        vs = vi % 2
                        rv = R("vst%d" % vs)
                        if vi % 2 == 0:
                            S.op("act", lambda e, vs=vs, pb=pb: e.activation(out=vst[vs][:], in_=PS[pb][:, 0:256], func=AF.Copy), reads=[RP[pb]], writes=[rv])
                        else:
                            S.op("dve", lambda e, vs=vs, pb=pb: e.tensor_copy(out=vst[vs][:], in_=PS[pb][:, 0:256]), reads=[RP[pb]], writes=[rv])
                        t0 = tb * 1024 + tt * 128
                        S.dma("sp", lambda e, vs=vs, t0=t0, c0=c0: e.dma_start(out=va_tok[t0:t0 + 128, c0 - 1536:c0 - 1536 + 256], in_=vst[vs][:]), reads=[rv], writes=[R("va_tok")])
                        vi += 1
                else:
                    for tt in range(8):
                        for k in range(32):
                            S.op("pe", lambda e, wb=wbufs[wi], k=k, tt=tt: e.matmul(PS[6][:, tt * 48:(tt + 1) * 48], hT[:, k, tt * 128:(tt + 1) * 128], wb[:, k, 0:48], start=(k == 0), stop=(k == 31)),
                                 reads=[rw, rh], writes=[RP[6]], inc=(k == 31 and tt == 7))
                    S.op("dve", lambda e, tb=tb: e.tensor_copy(out=bd_tok[:, tb * 8:(tb + 1) * 8, :].rearrange("p a b -> p (a b)"), in_=PS[6][:, 0:384]), reads=[RP[6]], writes=[RTAB])
        S.barrier()
        M.release(mk)

    def out_proj(l, xsrc, xdst):
        nonlocal wbufs
        mk = M.mark()
        yTb = M.alloc([128, 32, 1024], BF16)
        wbufs = [M.alloc([128, 32, 512], BF16) for _ in range(2)]
        xr = [M.alloc([128, 512], F32) for _ in range(4)]
        ry = R("yTb")
        yv = yT.rearrange("(k p) t -> p k t", p=128)
        it = 0
        for tb in range(2):
            for q4 in range(4):
                S.dma("sp", lambda e, tb=tb, q4=q4: e.dma_start(out=yTb[:, q4 * 8:(q4 + 1) * 8, :], in_=yv[:, q4 * 8:(q4 + 1) * 8, tb * 1024:(tb + 1) * 1024]), reads=[R("yT")], writes=[ry])
            for cb in range(8):
                wi = cb % 2
                rw = R("wb%d" % wi)
                load_wblock(w_out[l], cb * 512, 512, wi)
                for tt in range(8):
                    pb = it % 4
                    xi = it % 4
                    rx = R("xr%d" % xi)
                    t0 = tb * 1024 + tt * 128
                    S.dma("sp", lambda e, xi=xi, t0=t0, cb=cb: e.dma_start(out=xr[xi][:], in_=xsrc[t0:t0 + 128, cb * 512:(cb + 1) * 512]), writes=[rx])
                    for k in range(32):
                        S.op("pe", lambda e, wb=wbufs[wi], k=k, tt=tt, pb=pb: e.matmul(PS[pb][:], yTb[:, k, tt * 128:(tt + 1) * 128], wb[:, k, :], start=(k == 0), stop=(k == 31)),
                             reads=[rw, ry], writes=[RP[pb]], inc=(k == 31))
                    S.op("dve", lambda e, xi=xi, pb=pb: e.tensor_tensor(out=xr[xi][:], in0=PS[pb][:], in1=xr[xi][:], op=ALU.add), reads=[RP[pb], rx], writes=[rx])
                    S.dma("sp", lambda e, xi=xi, t0=t0, cb=cb: e.dma_start(out=xdst[t0:t0 + 128, cb * 512:(cb + 1) * 512], in_=xr[xi][:]), reads=[rx], writes=[R("xdst")])
                    it += 1
        S.barrier()
        M.release(mk)

    def final_norm(s):
        mk = M.mark()
        fg = M.alloc([128, D_MODEL], F32)
        xb = [M.alloc([128, D_MODEL], F32) for _ in range(2)]
        junk = M.alloc([128, D_MODEL], BF16)
        ss = [M.alloc([128, 1], F32) for _ in range(2)]
        S.dma("sp", lambda e: e.dma_start(out=fg[:], in_=final_g.partition_broadcast(128)), writes=[R("fg")])
        for t in range(NT):
            i = t % 2
            rx, rss = R("fx%d" % i), R("fss%d" % i)
            S.dma("sp", lambda e, t=t, i=i: e.dma_start(out=xb[i][:], in_=out[s, t * 128:(t + 1) * 128, :]), reads=[R("xdst")], writes=[rx])
            S.op("pool", lambda e, i=i: e.memset(ss[i][:], 0.0), writes=[rss])
            S.op("act", lambda e, i=i: e.activation(out=junk[:], in_=xb[i][:], func=AF.Square, accum_out=ss[i][:]), reads=[rx], writes=[R("fjunk"), rss])
            S.op("act", lambda e, i=i: e.activation(out=ss[i][:], in_=ss[i][:], func=AF.Sqrt, bias=eps_t[:], scale=1.0 / D_MODEL), reads=[rss, RC], writes=[rss])
            S.op("dve", lambda e, i=i: e.reciprocal(out=ss[i][:], in_=ss[i][:]), reads=[rss], writes=[rss])
            S.op("dve", lambda e, i=i: e.scalar_tensor_tensor(out=xb[i][:], in0=xb[i][:], scalar=ss[i][:], in1=fg[:], op0=ALU.mult, op1=ALU.mult), reads=[rx, rss, R("fg")], writes=[rx])
            S.dma("sp", lambda e, t=t, i=i: e.dma_start(out=out[s, t * 128:(t + 1) * 128, :], in_=xb[i][:]), reads=[rx], writes=[R("outfin")])
        S.barrier()
        M.release(mk)

    def branch_kvc(l, s):
        nonlocal wbufs
        mk = M.mark()
        mT = M.alloc([128, 32, 256], BF16)
        wbufs = [M.alloc([128, 32, 256], BF16) for _ in range(2)]
        kT = M.alloc([128, 8, 256], BF16)
        vtok = M.alloc([128, 2, 1024], BF16)
        rm, rk, rv = R("mT"), R("kT"), R("vtok")
        norm_transpose(lambda t: mem_in[s, t * 128:(t + 1) * 128, :], 2, mg_col, mT, 0, rm)
        it = 0
        for blk in range(8):
            wi = blk % 2
            rw = R("wb%d" % wi)
            load_wblock(w_mem_kv[l], blk * 256, 256, wi)
            for c in range(2):
                pb = it % 2
                it += 1
                if blk < 4:
                    tile = blk * 2 + c
                    for k in range(32):
                        S.op("pe", lambda e, wb=wbufs[wi], k=k, c=c, pb=pb: e.matmul(PS[pb][:, 0:256], wb[:, k, c * 128:(c + 1) * 128], mT[:, k, :], start=(k == 0), stop=(k == 31)),
                             reads=[rw, rm], writes=[RP[pb]], inc=(k == 31))
                    S.op("act", lambda e, pb=pb, tile=tile: e.activation(out=kT[:, tile, :], in_=PS[pb][:, 0:256], func=AF.Copy), reads=[RP[pb]], writes=[rk])
                else:
                    mt = c
                    for k in range(32):
                        S.op("pe", lambda e, wb=wbufs[wi], k=k, mt=mt, pb=pb: e.matmul(PS[pb][:, 0:256], mT[:, k, mt * 128:(mt + 1) * 128], wb[:, k, :], start=(k == 0), stop=(k == 31)),
                             reads=[rw, rm], writes=[RP[pb]], inc=(k == 31))
                    S.op("dve", lambda e, pb=pb, mt=mt, blk=blk: e.tensor_copy(out=vtok[:, mt, (blk - 4) * 256:(blk - 3) * 256], in_=PS[pb][:, 0:256]), reads=[RP[pb]], writes=[rv])
        dbg("kT", kT, [128, 8, 256], BF16, rk)
        dbg("vtok", vtok, [128, 2, 1024], BF16, rv)
        dbg("mT", mT, [128, 32, 256], BF16, rm)
        if "C" in phases:
            qc = [M.alloc([128, 8, 512], BF16) for _ in range(2)]
            gc = [M.alloc([128, 8, 512], F32) for _ in range(2)]
            eT = [M.alloc([128, 2, 512], BF16) for _ in range(2)]
            rden = [M.alloc([128, 512], F32) for _ in range(2)]
            tmp = [M.alloc([128, 512], F32) for _ in range(2)]
            yc = [M.alloc([128, 2, 512], BF16) for _ in range(2)]
            ih = 0
            for tb4 in range(4):
                i = tb4 % 2
                rq, rg = R("qc%d" % i), R("gcf%d" % i)
                t0 = tb4 * 512
                S.dma("pool", lambda e, i=i, t0=t0: e.dma_start(out=qc[i][:], in_=zT[10800:11824, t0:t0 + 512].rearrange("(j p) t -> p j t", p=128)), reads=[R("zT")], writes=[rq])
                S.dma("sp", lambda e, i=i, t0=t0: e.dma_start(out=gc[i][:], in_=zT[11824:12848, t0:t0 + 512].rearrange("(j p) t -> p j t", p=128)), reads=[R("zT")], writes=[rg])
                S.op("act", lambda e, i=i: e.activation(out=gc[i][:], in_=gc[i][:], func=AF.Silu), reads=[rg], writes=[rg])
                for h in range(4):
                    j = ih % 2
                    ih += 1
                    re_, rd, rt, ry = R("eT%d" % j), R("rden%d" % j), R("ctmp%d" % j), R("yc%d" % j)
                    for mt in range(2):
                        for dt in range(2):
                            S.op("pe", lambda e, h=h, dt=dt, mt=mt, i=i: e.matmul(PS[2 + mt][:], kT[:, h * 2 + dt, mt * 128:(mt + 1) * 128], qc[i][:, h * 2 + dt, :], start=(dt == 0), stop=(dt == 1)),
                                 reads=[rk, rq], writes=[RP[2 + mt]], inc=(dt == 1))
                        S.op("act", lambda e, j=j, mt=mt: e.activation(out=eT[j][:, mt, :], in_=PS[2 + mt][:], func=AF.Exp, scale=1.0 / 16.0), reads=[RP[2 + mt]], writes=[re_])
                    for mt in range(2):
                        S.op("pe", lambda e, j=j, mt=mt: e.matmul(PS[4][:], ones_b[:], eT[j][:, mt, :], start=(mt == 0), stop=(mt == 1)), reads=[re_, RC], writes=[RP[4]], inc=(mt == 1))
                    S.op("dve", lambda e, j=j: e.reciprocal(out=rden[j][:], in_=PS[4][:]), reads=[RP[4]], writes=[rd])
                    if tb4 == 0 and h == 0:
                        dbg("eT", eT[j], [128, 2, 512], BF16, re_)
                        dbg("rden", rden[j], [128, 512], F32, rd)
                        dbg("qc", qc[i], [128, 8, 512], BF16, rq)
                    for dt in range(2):
                        for mt in range(2):
                            S.op("pe", lambda e, j=j, h=h, dt=dt, mt=mt: e.matmul(PS[5 + dt][:], vtok[:, mt, h * 256 + dt * 128:h * 256 + (dt + 1) * 128], eT[j][:, mt, :], start=(mt == 0), stop=(mt == 1)),
                                 reads=[rv, re_], writes=[RP[5 + dt]], inc=(mt == 1))
                        S.op("dve", lambda e, j=j, dt=dt: e.tensor_tensor(out=tmp[j][:], in0=PS[5 + dt][:], in1=rden[j][:], op=ALU.mult), reads=[RP[5 + dt], rd], writes=[rt])
                        S.op("pool", lambda e, j=j, dt=dt, i=i, h=h: e.tensor_tensor(out=yc[j][:, dt, :], in0=tmp[j][:], in1=gc[i][:, h * 2 + dt, :], op=ALU.mult), reads=[rt, rg], writes=[ry])
                    S.dma("sp", lambda e, j=j, h=h, t0=t0: e.dma_start(out=yT[3072 + h * 256:3072 + (h + 1) * 256, t0:t0 + 512].rearrange("(dt p) t -> p dt t", p=128), in_=yc[j][:]), reads=[ry], writes=[R("yT")])
        S.barrier()
        M.release(mk)

    def branch_a(l, s):
        mk = M.mark()
        va = [M.alloc([128, 1536], F32) for _ in range(2)]
        uT = [M.alloc([128, 12, 128], F32) for _ in range(2)]
        ga = [M.alloc([128, 12, 128], F32) for _ in range(2)]
        nrm = [M.alloc([128, 1536], BF16) for _ in range(2)]
        mixed = [M.alloc([128, 12, 128], F32) for _ in range(2)]
        ya = [M.alloc([128, 12, 128], BF16) for _ in range(2)]
        st = [M.alloc([128, 8], F32) for _ in range(2)]
        for n in range(NT):
            i = n % 2
            rva, ru, rg, rn, rmx, rya, rst = R("va%d" % i), R("uT%d" % i), R("ga%d" % i), R("nrm%d" % i), R("mixed%d" % i), R("ya%d" % i), R("ast%d" % i)
            t0 = n * 128
            S.dma("sp", lambda e, i=i, t0=t0: e.dma_start(out=va[i][:], in_=va_tok[t0:t0 + 128, :]), reads=[R("va_tok")], writes=[rva])
            S.dma("sp", lambda e, i=i, t0=t0: e.dma_start(out=uT[i][:], in_=zT[0:1536, t0:t0 + 128].rearrange("(g c) t -> c g t", c=128)), reads=[R("zT")], writes=[ru])
            S.dma("sp", lambda e, i=i, t0=t0: e.dma_start(out=ga[i][:], in_=zT[3072:4608, t0:t0 + 128].rearrange("(g c) t -> c g t", c=128)), reads=[R("zT")], writes=[rg])
            sm = st[i]
            S.op("pool", lambda e, sm=sm: e.memset(sm[:], 0.0), writes=[rst])
            S.op("act", lambda e, i=i, sm=sm: e.activation(out=va[i][:], in_=va[i][:], func=AF.Gelu_apprx_tanh, accum_out=sm[:, 0:1]), reads=[rva, rst], writes=[rva, rst])
            S.op("act", lambda e, i=i, sm=sm: e.activation(out=nrm[i][:], in_=va[i][:], func=AF.Square, accum_out=sm[:, 1:2]), reads=[rva, rst], writes=[rn, rst])
            S.op("dve", lambda e, sm=sm: e.tensor_scalar(out=sm[:, 2:3], in0=sm[:, 0:1], scalar1=1.0 / 1536, scalar2=0.0, op0=ALU.mult, op1=ALU.add), reads=[rst], writes=[rst])
            S.op("dve", lambda e, sm=sm: e.tensor_tensor(out=sm[:, 3:4], in0=sm[:, 2:3], in1=sm[:, 2:3], op=ALU.mult), reads=[rst], writes=[rst])
            S.op("dve", lambda e, sm=sm: e.scalar_tensor_tensor(out=sm[:, 4:5], in0=sm[:, 1:2], scalar=1.0 / 1536, in1=sm[:, 3:4], op0=ALU.mult, op1=ALU.subtract), reads=[rst], writes=[rst])
            S.op("act", lambda e, sm=sm: e.activation(out=sm[:, 5:6], in_=sm[:, 4:5], func=AF.Sqrt, bias=eps_t[:], scale=1.0), reads=[rst, RC], writes=[rst])
            S.op("dve", lambda e, sm=sm: e.reciprocal(out=sm[:, 6:7], in_=sm[:, 5:6]), reads=[rst], writes=[rst])
            S.op("dve", lambda e, i=i, sm=sm: e.tensor_scalar(out=nrm[i][:], in0=va[i][:], scalar1=sm[:, 2:3], scalar2=sm[:, 6:7], op0=ALU.subtract, op1=ALU.mult), reads=[rva, rst], writes=[rn])
            pb0 = (n % 2) * 3
            for g in range(12):
                pb = pb0 + g // 4
                S.op("pe", lambda e, i=i, g=g, pb=pb: e.matmul(PS[pb][:, (g % 4) * 128:(g % 4 + 1) * 128], nrm[i][:, g * 128:(g + 1) * 128], wsT[:, g, :], start=True, stop=True),
                     reads=[rn, RPAR], writes=[RP[pb]], inc=(g % 4 == 3))
            for g in range(12):
                pb = pb0 + g // 4
                S.op("dve", lambda e, i=i, g=g, pb=pb: e.scalar_tensor_tensor(out=mixed[i][:, g, :], in0=PS[pb][:, (g % 4) * 128:(g % 4 + 1) * 128], scalar=lng_col[:, g:g + 1], in1=Rsg[:, g, :], op0=ALU.mult, op1=ALU.add),
                     reads=[RP[pb], RPAR], writes=[rmx])
            S.op("act", lambda e, i=i: e.activation(out=uT[i][:], in_=uT[i][:], func=AF.Gelu_apprx_tanh), reads=[ru], writes=[ru])
            S.op("act", lambda e, i=i: e.activation(out=ga[i][:], in_=ga[i][:], func=AF.Silu), reads=[rg], writes=[rg])
            S.op("pool", lambda e, i=i: e.tensor_tensor(out=mixed[i][:], in0=mixed[i][:], in1=uT[i][:], op=ALU.mult), reads=[rmx, ru], writes=[rmx])
            S.op("pool", lambda e, i=i: e.tensor_tensor(out=ya[i][:], in0=mixed[i][:], in1=ga[i][:], op=ALU.mult), reads=[rmx, rg], writes=[rya])
            S.dma("sp", lambda e, i=i, t0=t0: e.dma_start(out=yT[0:1536, t0:t0 + 128].rearrange("(g c) t -> c g t", c=128), in_=ya[i][:]), reads=[rya], writes=[R("yT")])
        S.barrier()
        M.release(mk)

    def branch_b(l, s):
        mk = M.mark()
        beta_t = M.alloc([128, 2, NT, 12], F32)
        g_t = M.alloc([128, 2, NT, 12], F32)
        gc_t = M.alloc([128, 2, NT, 12], F32)
        bw_t = M.alloc([128, 2, NT, 12], F32)
        kd_t = M.alloc([128, 2, NT, 12], F32)
        egl_t = M.alloc([128, 2, NT, 12], F32)
        gcT = [M.alloc([12, SEQ], F32) for _ in range(2)]
        egcT = [M.alloc([12, SEQ], BF16) for _ in range(2)]
        tA = M.alloc([128, NT, 24], F32)
        tB = M.alloc([128, NT, 24], F32)
        RT = R("gtab")
        flat = lambda t: t[:].rearrange("p d m h -> p (d m h)")
        pm = lambda t: t[:].rearrange("p d m h -> p m d h")
        S.op("act", lambda e: e.activation(out=tA[:], in_=bd_tok[:, :, 0:24], func=AF.Exp, scale=-1.0), reads=[RTAB], writes=[RT])
        S.op("dve", lambda e: e.tensor_scalar(out=tA[:], in0=tA[:], scalar1=1.0, scalar2=0.0, op0=ALU.add, op1=ALU.add), reads=[RT], writes=[RT])
        S.op("dve", lambda e: e.reciprocal(out=pm(beta_t), in_=tA[:].rearrange("p m (d h) -> p m d h", d=2)), reads=[RT], writes=[RT])
        S.op("dve", lambda e: e.tensor_tensor(out=tB[:], in0=bd_tok[:, :, 24:48], in1=dtb[:].unsqueeze(1).to_broadcast([128, NT, 24]), op=ALU.add), reads=[RTAB, RPAR], writes=[RT])
        S.op("act", lambda e: e.activation(out=tB[:], in_=tB[:], func=AF.Exp), reads=[RT], writes=[RT])
        S.op("act", lambda e: e.activation(out=tB[:], in_=tB[:], func=AF.Ln, bias=one_t[:], scale=1.0), reads=[RT, RC], writes=[RT])
        S.op("dve", lambda e: e.tensor_tensor(out=pm(g_t), in0=tB[:].rearrange("p m (d h) -> p m d h", d=2), in1=negA[:].rearrange("p (d h) -> p d h", d=2).unsqueeze(1).to_broadcast([128, NT, 2, 12]), op=ALU.mult), reads=[RT, RPAR], writes=[RT])
        for d in range(2):
            S.op("pe", lambda e, d=d: e.matmul(PS[0][:, d * 192:(d + 1) * 192], (Ltri if d == 0 else Utri)[:], g_t[:, d].rearrange("p m h -> p (m h)"), start=True, stop=True), reads=[RT, RC], writes=[RP[0]])
        S.op("dve", lambda e: e.tensor_copy(out=flat(gc_t), in_=PS[0][:, 0:384]), reads=[RP[0]], writes=[RT])
        S.op("pe", lambda e: e.matmul(PS[1][:, 0:384], ones_f[:], flat(g_t), start=True, stop=True), reads=[RT, RC], writes=[RP[1]])
        S.op("dve", lambda e: e.tensor_tensor(out=flat(kd_t), in0=PS[1][:, 0:384], in1=flat(gc_t), op=ALU.subtract), reads=[RP[1], RT], writes=[RT])
        S.op("dve", lambda e: e.tensor_copy(out=flat(egl_t), in_=PS[1][:, 0:384]), reads=[RP[1]], writes=[RT])
        S.op("act", lambda e: e.activation(out=flat(egl_t), in_=flat(egl_t), func=AF.Exp), reads=[RT], writes=[RT])
        S.op("act", lambda e: e.activation(out=flat(kd_t), in_=flat(kd_t), func=AF.Exp), reads=[RT], writes=[RT])
        S.op("act", lambda e: e.activation(out=flat(bw_t), in_=flat(gc_t), func=AF.Exp), reads=[RT], writes=[RT])
        S.op("dve", lambda e: e.tensor_tensor(out=flat(bw_t), in0=flat(bw_t), in1=flat(beta_t), op=ALU.mult), reads=[RT], writes=[RT])
        for d in range(2):
            for m in range(NT):
                pb = 2 + (m // 4) % 2
                S.op("pe", lambda e, d=d, m=m, pb=pb: e.matmul(PS[pb][0:12, (m % 4) * 128:(m % 4 + 1) * 128], g_t[:, d, m, :], (Ltri if d == 0 else Utri)[:], start=True, stop=True),
                     reads=[RT, RC], writes=[RP[pb]], inc=(m % 4 == 3))
                if m % 4 == 3:
                    S.op("dve", lambda e, d=d, m=m, pb=pb: e.tensor_copy(out=gcT[d][:, (m - 3) * 128:(m + 1) * 128], in_=PS[pb][0:12, :]), reads=[RP[pb]], writes=[RT])
            S.op("act", lambda e, d=d: e.activation(out=egcT[d][:], in_=gcT[d][:], func=AF.Exp), reads=[RT], writes=[RT])

        if "Bstop1" in phases:
            S.barrier()
            M.release(mk)
            return
        raw = [M.alloc([128, SEQ + 4], F32) for _ in range(2)]
        acc = M.alloc([128, SEQ], F32)
        sqb = M.alloc([128, SEQ], BF16)
        qT = M.alloc([128, SEQ], BF16)
        kT = M.alloc([128, SEQ], BF16)
        vT = M.alloc([128, SEQ], BF16)
        qgT = [M.alloc([128, SEQ], BF16) for _ in range(2)]
        sgate = M.alloc([128, SEQ], BF16)
        rsd = [M.alloc([128, 512], F32) for _ in range(2)]
        kdm = [M.alloc([128, NT, 128], BF16) for _ in range(2)]
        u_ = [M.alloc([128, NT, 128], F32) for _ in range(2)]
        nwT = [M.alloc([128, NT, 128], BF16) for _ in range(2)]
        qkT = [M.alloc([128, NT, 128], BF16) for _ in range(2)]
        UB = []
        for _d in range(2):
            f = lambda: M.alloc([128, 2, 128], F32)
            fr = lambda: M.alloc([128, 2, 128], F32R)
            fb = lambda: M.alloc([128, 2, 128], BF16)
            UB.append((f(), f(), fr(), fr(), fr(), [fr(), fr()], [fr(), fr()], fb(), fb(), fb(), fb()))
        o_d = [M.alloc([128, NT, 128], F32) for _ in range(2)]
        Sst = [M.alloc([128, 128], F32) for _ in range(2)]
        Sbf = [M.alloc([128, 128], BF16) for _ in range(2)]
        vnew = [M.alloc([128, 128], BF16) for _ in range(2)]
        ssn = M.alloc([128, NT], F32)
        eps128 = M.alloc([128, 1], F32)
        ident_r = M.alloc([128, 128], F32R)
        S.op("pool", lambda e: e.tensor_copy(out=ident_r[:], in_=ident_f[:]), reads=[RC], writes=[RT])
        Sel12 = M.alloc([12, 12, 128], F32)
        Sel12b = M.alloc([12, 12, 128], BF16)
        S.op("pool", lambda e: e.memset(eps128[:], 128.0 * EPS), writes=[RT])
        S.op("pool", lambda e: e.tensor_copy(out=Sel12[:], in_=ident_f[0:12, 0:12].unsqueeze(2).to_broadcast([12, 12, 128])), reads=[RC], writes=[RT])
        S.op("pool", lambda e: e.tensor_copy(out=Sel12b[:], in_=Sel12[:]), reads=[RT], writes=[RT])
        for i in range(2):
            S.op("pool", lambda e, i=i: e.memset(raw[i][:, 0:2], 0.0), writes=[R("raw%d" % i)])
            S.op("pool", lambda e, i=i: e.memset(raw[i][:, SEQ + 2:SEQ + 4], 0.0), writes=[R("raw%d" % i)])
        Racc, Rsq, RqT, RkT, RvT, Rsg_, Rkdm = R("acc"), R("sqb"), R("qT"), R("kT"), R("vT"), R("sgate"), R("kdm")
        Rqg = [R("qgT0"), R("qgT1")]
        ri = [0]
        def prep1(h):
            for which, (row0, dstT, rdst) in enumerate(((4608, qT, RqT), (6144, kT, RkT), (7680, vT, RvT))):
                i = ri[0] % 2
                ri[0] += 1
                rr = R("raw%d" % i)
                r0 = row0 + h * 128
                S.dma("sp", lambda e, i=i, r0=r0: e.dma_start(out=raw[i][:, 2:SEQ + 2], in_=zT[r0:r0 + 128, :]), reads=[R("zT")], writes=[rr])
                ti = which * 12 + h
                S.op("act", lambda e, i=i, ti=ti: e.activation(out=acc[:], in_=raw[i][:, 0:SEQ], func=AF.Identity, scale=cw_col[:, ti, 0:1]), reads=[rr, RPAR], writes=[Racc])
                ceng = "dve"
                for j in range(1, 5):
                    S.op(ceng, lambda e, i=i, ti=ti, j=j: e.scalar_tensor_tensor(out=acc[:], in0=raw[i][:, j:j + SEQ], scalar=cw_col[:, ti, j:j + 1], in1=acc[:], op0=ALU.mult, op1=ALU.add), reads=[rr, RPAR, Racc], writes=[Racc])
                    yield
                if which == 2:
                    S.op("act", lambda e: e.activation(out=vT[:], in_=acc[:], func=AF.Silu), reads=[Racc], writes=[RvT])
                else:
                    S.op("act", lambda e: e.activation(out=acc[:], in_=acc[:], func=AF.Silu), reads=[Racc], writes=[Racc])
                    S.op("pool", lambda e: e.tensor_tensor(out=sqb[:], in0=acc[:], in1=acc[:], op=ALU.mult), reads=[Racc], writes=[Rsq])
                    for c4 in range(4):
                        pb = 6 + c4 % 2
                        j = c4 % 2
                        rrs = R("rsd%d" % j)
                        S.op("pe", lambda e, c4=c4, pb=pb: e.matmul(PS[pb][:], ones_b[:], sqb[:, c4 * 512:(c4 + 1) * 512], start=True, stop=True), reads=[Rsq, RC], writes=[RP[pb]])
                        if which == 0:
                            S.op("act", lambda e, pb=pb, j=j: e.activation(out=rsd[j][:], in_=PS[pb][:], func=AF.Ln, bias=eps128[:], scale=128.0), reads=[RP[pb], RT], writes=[rrs])
                        else:
                            S.op("act", lambda e, pb=pb, j=j: e.activation(out=rsd[j][:], in_=PS[pb][:], func=AF.Ln, bias=eps_t[:], scale=1.0), reads=[RP[pb], RC], writes=[rrs])
                        S.op("act", lambda e, j=j: e.activation(out=rsd[j][:], in_=rsd[j][:], func=AF.Exp, scale=-0.5), reads=[rrs], writes=[rrs])
                        S.op("dve", lambda e, j=j, c4=c4, dstT=dstT: e.tensor_tensor(out=dstT[:, c4 * 512:(c4 + 1) * 512], in0=acc[:, c4 * 512:(c4 + 1) * 512], in1=rsd[j][:], op=ALU.mult), reads=[Racc, rrs], writes=[rdst])
                        yield
        def prep2(h):
            i = ri[0] % 2
            ri[0] += 1
            rr = R("raw%d" % i)
            r0 = 9216 + h * 128
            S.dma("sp", lambda e, i=i, r0=r0: e.dma_start(out=raw[i][:, 2:SEQ + 2], in_=zT[r0:r0 + 128, :]), reads=[R("zT")], writes=[rr])
            S.op("act", lambda e, i=i: e.activation(out=sgate[:], in_=raw[i][:, 2:SEQ + 2], func=AF.Silu), reads=[rr], writes=[Rsg_])
            for d in range(2):
                for c4 in range(4):
                    pb = 2 + c4 % 2
                    S.op("pe", lambda e, d=d, c4=c4, pb=pb, h=h: e.matmul(PS[pb][:], Sel12b[:, h, :], egcT[d][:, c4 * 512:(c4 + 1) * 512], start=True, stop=True), reads=[RT], writes=[RP[pb]])
                    S.op("dve", lambda e, d=d, c4=c4, pb=pb: e.tensor_tensor(out=qgT[d][:, c4 * 512:(c4 + 1) * 512], in0=PS[pb][:], in1=qT[:, c4 * 512:(c4 + 1) * 512], op=ALU.mult), reads=[RP[pb], RqT], writes=[Rqg[d]])
        def phase1(h):
            def unit(d, G, m0, hb7, h=h):
                B1, B2, B3 = 1 + 3 * d, 2 + 3 * d, 3 + 3 * d
                RD, RDs, RA, RAT, RTT, Rqk, RTb, Rvb, Rkw = (R(n + str(d)) for n in ("Dp", "Ds", "Am", "ATm", "TT", "qkb", "TTb", "vbg", "kwg"))
                Dp, Ds, Am, ATm, TT, Pn, PTn, qkb, TTb, vbg, kwg = UB[d]
                trv = phb(7, hb7)
                for mi in range(2):
                    m = m0 + mi
                    S.op("act", lambda e, mi=mi, m=m: e.activation(out=vbg[:, mi, :], in_=trv[:, 2 + mi, :], func=AF.Identity, scale=beta_t[:, d, m, h:h + 1]), reads=[RH(7, hb7), RT], writes=[Rvb], defer=True)
                    S.op("act", lambda e, mi=mi, m=m: e.activation(out=kwg[:, mi, :], in_=trv[:, mi, :], func=AF.Identity, scale=bw_t[:, d, m, h:h + 1]), reads=[RH(7, hb7), RT], writes=[Rkw], defer=True)
                    S.op("act", lambda e, mi=mi, m=m: e.activation(out=kdm[d][:, m, :], in_=trv[:, mi, :], func=AF.Identity, scale=kd_t[:, d, m, h:h + 1]), reads=[RH(7, hb7), RT], writes=[Rkdm], defer=(mi == 0))
                S.op("pe", lambda e: e.matmul(PS[B2][:, 0:256], Sel12[:, h, :], gcT[d][:, m0 * 128:(m0 + 2) * 128], start=True, stop=False), reads=[RT], writes=[RH(B2, 0)], inc=False)
                S.op("pe", lambda e: e.matmul(PS[B2][:, 0:256], ident_f[:], BIG[d][:, 0:2, :].rearrange("p a b -> p (a b)"), start=False, stop=True), reads=[RC], writes=[RH(B2, 0)])
                yield
                for mi in range(2):
                    m = m0 + mi
                    S.op("act", lambda e, mi=mi, m=m: e.activation(out=Dp[:, mi, :], in_=ph(B2, 0)[:, mi, :], func=AF.Exp, bias=gc_t[:, d, m, h:h + 1], scale=-1.0), reads=[RH(B2, 0), RT], writes=[RD], defer=(mi == 0))
                yield
                S.op("pool", lambda e: e.tensor_tensor(out=Ds[:], in0=Dp[:], in1=ST01[d][:].unsqueeze(1).to_broadcast([128, 2, 128]), op=ALU.mult), reads=[RD, RC], writes=[RDs])
                yield
                for mi in range(2):
                    m = m0 + mi
                    S.op("dve", lambda e, mi=mi, m=m: e.scalar_tensor_tensor(out=Am[:, mi, :], in0=ph(0, 0)[:, mi, :], scalar=beta_t[:, d, m, h:h + 1], in1=Ds[:, mi, :], op0=ALU.mult, op1=ALU.mult), reads=[RH(0, 0), RT, RDs], writes=[RA], defer=True)
                S.op("dve", lambda e: e.tensor_tensor(out=qkb[:], in0=ph(0, 1), in1=Dp[:], op=ALU.mult), reads=[RH(0, 1), RD], writes=[Rqk])
                yield
                for mi in range(2):
                    S.op("pe", lambda e, mi=mi: e.matmul(ph(B1, 0)[:, mi, :], Am[:, mi, :], ident_r[:], start=True, stop=True), reads=[RA, RT], writes=[RH(B1, 0)], inc=(mi == 1))
                for mi in range(2):
                    S.op("pe", lambda e, mi=mi: e.transpose(phb(B3, 0)[:, mi, :], qkb[:, mi, :], ident_b[:]), reads=[Rqk, RC], writes=[RH(B3, 0)], inc=(mi == 1))
                yield
                S.op("dve", lambda e: e.tensor_copy(out=ATm[:], in_=ph(B1, 0)), reads=[RH(B1, 0)], writes=[RAT], defer=True)
                S.op("dve", lambda e: e.tensor_tensor(out=TT[:], in0=ident_f[:].unsqueeze(1).to_broadcast([128, 2, 128]), in1=ph(B1, 0), op=ALU.subtract), reads=[RH(B1, 0), RC], writes=[RTT])
                S.op("act", lambda e: e.activation(out=qkT[d][:, m0:m0 + 2, :], in_=phb(B3, 0)[:, 0:2, :], func=AF.Copy), reads=[RH(B3, 0)], writes=[R("qkT%d" % d)])
                yield
                Pc, PTc, Rp, Rpt = Am, ATm, RA, RAT
                for lev in range(1, 7):
                    j = lev % 2
                    RPn, RPTn = R("Pn%d_%d" % (j, d)), R("PTn%d_%d" % (j, d))
                    for mi in range(2):
                        S.op("pe", lambda e, mi=mi, Pc=Pc, PTc=PTc: e.matmul(ph(B2, 0)[:, mi, :], PTc[:, mi, :].bitcast(F32R), Pc[:, mi, :].bitcast(F32R), start=True, stop=True), reads=[Rp, Rpt], writes=[RH(B2, 0)], inc=(mi == 1))
                    if lev < 6:
                        for mi in range(2):
                            S.op("pe", lambda e, mi=mi, Pc=Pc, PTc=PTc: e.matmul(ph(B3, 1)[:, mi, :], Pc[:, mi, :].bitcast(F32R), PTc[:, mi, :].bitcast(F32R), start=True, stop=True), reads=[Rp, Rpt], writes=[RH(B3, 1)], inc=(mi == 1))
                    yield
                    S.op("dve", lambda e, j=j: e.tensor_copy(out=Pn[j][:], in_=ph(B2, 0)), reads=[RH(B2, 0)], writes=[RPn])
                    if lev < 6:
                        S.op("act", lambda e, j=j: e.activation(out=PTn[j][:], in_=ph(B3, 1), func=AF.Copy), reads=[RH(B3, 1)], writes=[RPTn])
                    yield
                    for mi in range(2):
                        S.op("pe", lambda e, mi=mi, j=j: e.matmul(ph(B1, 0)[:, mi, :], Pn[j][:, mi, :].bitcast(F32R), TT[:, mi, :].bitcast(F32R), start=True, stop=True), reads=[RPn, RTT], writes=[RH(B1, 0)], inc=(mi == 1))
                    yield
                    if lev < 6:
                        S.op("dve", lambda e: e.tensor_tensor(out=TT[:], in0=TT[:], in1=ph(B1, 0), op=ALU.add), reads=[RTT, RH(B1, 0)], writes=[RTT])
                    else:
                        S.op("dve", lambda e: e.tensor_tensor(out=TTb[:], in0=TT[:], in1=ph(B1, 0), op=ALU.add), reads=[RTT, RH(B1, 0)], writes=[RTb])
                    yield
                    Pc, PTc, Rp, Rpt = Pn[j], PTn[j], RPn, RPTn
                for mi in range(2):
                    S.op("pe", lambda e, mi=mi: e.matmul(ph(B2, 0)[:, mi, :], TTb[:, mi, :], vbg[:, mi, :], start=True, stop=True), reads=[RTb, Rvb], writes=[RH(B2, 0)], inc=(mi == 1))
                for mi in range(2):
                    S.op("pe", lambda e, mi=mi: e.matmul(ph(B3, 1)[:, mi, :], kwg[:, mi, :], TTb[:, mi, :], start=True, stop=True), reads=[RTb, Rkw], writes=[RH(B3, 1)], inc=(mi == 1))
                yield
                S.op("act", lambda e: e.activation(out=u_[d][:, m0:m0 + 2, :], in_=ph(B2, 0), func=AF.Copy), reads=[RH(B2, 0)], writes=[R("u%d" % d)], defer=True)
                S.op("act", lambda e: e.activation(out=nwT[d][:, m0:m0 + 2, :], in_=ph(B3, 1), func=AF.Identity, scale=-1.0), reads=[RH(B3, 1)], writes=[R("nwT%d" % d)])
                yield

            for G in range(8):
                m0 = G * 2
                hb7 = G % 2
                trv = phb(7, hb7)
                for mi in range(2):
                    S.op("pe", lambda e, mi=mi, trv=trv, m0=m0: e.transpose(trv[:, mi, :], kT[:, (m0 + mi) * 128:(m0 + mi + 1) * 128], ident_b[:]), reads=[RkT, RC], writes=[RH(7, hb7)], inc=False)
                for mi in range(2):
                    S.op("pe", lambda e, mi=mi, trv=trv, m0=m0: e.transpose(trv[:, 2 + mi, :], vT[:, (m0 + mi) * 128:(m0 + mi + 1) * 128], ident_b[:]), reads=[RvT, RC], writes=[RH(7, hb7)], inc=(mi == 1))
                for mi in range(2):
                    sl = slice((m0 + mi) * 128, (m0 + mi + 1) * 128)
                    S.op("pe", lambda e, mi=mi, sl=sl: e.matmul(ph(0, 0)[:, mi, :], kT[:, sl], kT[:, sl], start=True, stop=True), reads=[RkT], writes=[RH(0, 0)], inc=(mi == 1))
                for mi in range(2):
                    sl = slice((m0 + mi) * 128, (m0 + mi + 1) * 128)
                    S.op("pe", lambda e, mi=mi, sl=sl: e.matmul(ph(0, 1)[:, mi, :], qT[:, sl], kT[:, sl], start=True, stop=True), reads=[RqT, RkT], writes=[RH(0, 1)], inc=(mi == 1))
                gens = [unit(0, G, m0, hb7), unit(1, G, m0, hb7)]
                while gens:
                    for g in list(gens):
                        try:
                            next(g)
                        except StopIteration:
                            gens.remove(g)
        def phase2(h):
            for d in range(2):
                S.op("pool", lambda e, d=d: e.memset(Sst[d][:], 0.0), writes=[R("S%d" % d)])
                S.op("pool", lambda e, d=d: e.memset(Sbf[d][:], 0.0), writes=[R("Sbf%d" % d)])
            for step in range(NT):
                for d in range(2):
                    m = step if d == 0 else NT - 1 - step
                    b0 = d * 3
                    RS, RSb, Rvn, Ro = R("S%d" % d), R("Sbf%d" % d), R("vnew%d" % d), R("o%d" % d)
                    Ru, Rnw, Rqk2 = R("u%d" % d), R("nwT%d" % d), R("qkT%d" % d)
                    S.op("pe", lambda e, d=d, m=m, b0=b0: e.matmul(PS[b0][:, 0:128], nwT[d][:, m, :], Sbf[d][:], start=True, stop=True), reads=[Rnw, RSb], writes=[RP[b0]])
                    S.op("dve", lambda e, d=d, m=m, b0=b0: e.tensor_tensor(out=vnew[d][:], in0=PS[b0][:, 0:128], in1=u_[d][:, m, :], op=ALU.add), reads=[RP[b0], Ru], writes=[Rvn])
                    S.op("pe", lambda e, d=d, m=m, b0=b0: e.matmul(PS[b0 + 1][:, 0:128], qgT[d][:, m * 128:(m + 1) * 128], Sbf[d][:], start=True, stop=False), reads=[Rqg[d], RSb], writes=[RP[b0 + 1]], inc=False)
                    S.op("pe", lambda e, d=d, m=m, b0=b0: e.matmul(PS[b0 + 1][:, 0:128], qkT[d][:, m, :], vnew[d][:], start=False, stop=True), reads=[Rqk2, Rvn], writes=[RP[b0 + 1]])
                    S.op("act", lambda e, d=d, m=m, b0=b0: e.activation(out=o_d[d][:, m, :], in_=PS[b0 + 1][:, 0:128], func=AF.Copy), reads=[RP[b0 + 1]], writes=[Ro])
                    if step < NT - 1:
                        S.op("pe", lambda e, d=d, m=m, b0=b0: e.matmul(PS[b0 + 2][:, 0:128], kdm[d][:, m, :], vnew[d][:], start=True, stop=True), reads=[Rkdm, Rvn], writes=[RP[b0 + 2]])
                        S.op("dve", lambda e, d=d, m=m, b0=b0, h=h: e.scalar_tensor_tensor(out=Sst[d][:], in0=Sst[d][:], scalar=egl_t[:, d, m, h:h + 1], in1=PS[b0 + 2][:, 0:128], op0=ALU.mult, op1=ALU.add), reads=[RS, RT, RP[b0 + 2]], writes=[RS])
                        S.op("act", lambda e, d=d: e.activation(out=Sbf[d][:], in_=Sst[d][:], func=AF.Copy), reads=[RS], writes=[RSb])
                    yield
        def output(h):
            Ro0, Ro1 = R("o0"), R("o1")
            S.op("pool", lambda e: e.tensor_tensor(out=o_d[0][:], in0=o_d[0][:], in1=o_d[1][:], op=ALU.add), reads=[Ro0, Ro1], writes=[Ro0])
            S.op("pool", lambda e: e.tensor_tensor(out=o_d[1][:], in0=o_d[0][:], in1=o_d[0][:], op=ALU.mult), reads=[Ro0, Ro1], writes=[Ro1])
            S.op("dve", lambda e: e.tensor_reduce(out=ssn[:], in_=o_d[1][:], axis=AX.X, op=ALU.add), reads=[Ro1], writes=[R("ssn")])
            S.op("act", lambda e: e.activation(out=ssn[:], in_=ssn[:], func=AF.Sqrt, bias=eps_t[:], scale=1.0 / 128.0), reads=[R("ssn"), RC], writes=[R("ssn")])
            S.op("dve", lambda e: e.reciprocal(out=ssn[:], in_=ssn[:]), reads=[R("ssn")], writes=[R("ssn")])
            on = sqb[:].rearrange("p (m c) -> p m c", c=128)
            S.op("dve", lambda e: e.tensor_tensor(out=on, in0=o_d[0][:], in1=ssn[:].unsqueeze(2).to_broadcast([128, NT, 128]), op=ALU.mult), reads=[Ro0, R("ssn")], writes=[Rsq])
            for hb in range(2):
                for mi in range(8):
                    S.op("pe", lambda e, hb=hb, mi=mi: e.transpose(psb(6 + hb)[:, mi, :], on[:, hb * 8 + mi, :], ident_b[:]), reads=[Rsq, RC], writes=[RP[6 + hb]], inc=(mi == 7))
                S.op("dve", lambda e, hb=hb: e.scalar_tensor_tensor(out=sgate[:, hb * 1024:(hb + 1) * 1024], in0=psb(6 + hb).rearrange("p a b -> p (a b)"), scalar=gn_col[:, 0:1], in1=sgate[:, hb * 1024:(hb + 1) * 1024], op0=ALU.mult, op1=ALU.mult),
                     reads=[RP[6 + hb], RPAR, Rsg_], writes=[Rsg_])
            r0 = 1536 + h * 128
            S.dma("sp", lambda e, r0=r0: e.dma_start(out=yT[r0:r0 + 128, :], in_=sgate[:]), reads=[Rsg_], writes=[R("yT")])
        def drain(g):
            for _ in g:
                pass

        def interleave(ga, gb, ratio):
            a_live, b_live = True, gb is not None
            while a_live or b_live:
                if a_live:
                    try:
                        next(ga)
                    except StopIteration:
                        a_live = False
                if b_live:
                    for _ in range(ratio):
                        try:
                            next(gb)
                        except StopIteration:
                            b_live = False
                            break

        hl = list(heads)
        drain(prep1(hl[0]))
        for idx, h in enumerate(hl):
            prep2(h)
            if "Bstop2" in phases:
                break
            phase1(h)
            if "Bstop3" in phases:
                break
            nxt = prep1(hl[idx + 1]) if idx + 1 < len(hl) else None
            interleave(phase2(h), nxt, 2)
            output(h)
        S.barrier()
        M.release(mk)

    BR = {}
    BR['B'] = branch_b
    BR['KV'] = branch_kvc
    BR['A'] = branch_a
    exec_hooks = {}

    with nc.Block() as block:
        for l in layers:
            load_layer_params(l)
            for s in range(NSEQ):
                xsrc = x_in[s] if l == 0 else x1[s]
                xdst = x1[s] if l == 0 else out[s]
                if "P2" in phases:
                    in_proj(l, xsrc)
                for name in ("KV", "TAB", "A", "B", "C"):
                    if name in phases and name in BR:
                        BR[name](l, s)
                if "OUT" in phases:
                    out_proj(l, xsrc, xdst)
        if "FIN" in phases:
            for s in range(NSEQ):
                final_norm(s)
        S.emit(block)
    nc._sched_stats = (S.n_ops, M.peak)
    return nc


_NC_CACHE = {}


def kernel(**inputs):
    xp = np.asarray(inputs["x_prompt"], np.float32)
    xs = np.asarray(inputs["x_sample"], np.float32)
    mp = np.asarray(inputs["mem_prompt"], np.float32)
    ms = np.asarray(inputs["mem_sample"], np.float32)
    seqs = [("p", i) for i in range(4)] + [("s", i) for i in range(8)]

    def get(kind, i):
        return (xp[i], mp[i]) if kind == "p" else (xs[i], ms[i])

    slots = [[seqs[c], seqs[8 + c % 4]] for c in range(8)]
    wnames = ["norm_g", "w_in", "sgu_ln_g", "sgu_ln_b", "sgu_w", "sgu_b", "conv_w", "a_log", "dt_bias", "gdn_norm_g", "mem_norm_g", "w_mem_kv", "w_out", "final_g"]
    wts = {k: np.ascontiguousarray(np.asarray(inputs[k], np.float32)) for k in wnames}
    in_maps = []
    for c in range(8):
        xa = np.stack([get(*slots[c][j])[0] for j in range(2)])
        ma = np.stack([get(*slots[c][j])[1] for j in range(2)])
        m = {"x": xa, "mem": ma}
        m.update(wts)
        in_maps.append(m)
    if "nc" not in _NC_CACHE:
        _NC_CACHE["nc"] = build_program()
    res = run_bass_kernel_spmd(_NC_CACHE["nc"], in_maps, core_ids=list(range(8)))
    yp = np.empty_like(xp)
    ys = np.empty_like(xs)
    for c in range(8):
        o = res.results[c]["out"]
        for j in range(2):
            if j == 1 and c >= 4:
                continue
            kind, i = slots[c][j]
            if kind == "p":
                yp[i] = o[j]
            else:
                ys[i] = o[j]
    return (yp, ys)
```

```python
import numpy as np
import concourse.bass as bass
import concourse.mybir as mybir
from concourse.bass_utils import run_bass_kernel_spmd

F32 = mybir.dt.float32
BF16 = mybir.dt.bfloat16
F32R = mybir.dt.float32r
AF = mybir.ActivationFunctionType
ALU = mybir.AluOpType
AX = mybir.AxisListType

D_MODEL = 4096
SEQ = 2048
NT = SEQ // 128
N_IN = 12848
EPS = 1e-6
BIGM = 30000.0
import os
DBG_SERIAL = os.environ.get('DBG_SERIAL') == '1'


class Res:
    __slots__ = ("w", "r")

    def __init__(self):
        self.w = None
        self.r = []


class Sched:
    ENGS = ("pe", "act", "dve", "pool", "sp")
    NDMA = {"sp": 16, "act": 6, "pool": 8}

    def __init__(self, nc):
        self.nc = nc
        self.ops = {e: [] for e in self.ENGS}
        self.sem = {e: nc.alloc_semaphore(name="s_" + e) for e in self.ENGS}
        self.cnt = {e: 0 for e in self.ENGS}
        self.waited = {e: {} for e in self.ENGS}
        self.dsem = {q: [nc.alloc_semaphore(name="d_%s%d" % (q, i)) for i in range(n)] for q, n in self.NDMA.items()}
        self.dcnt = {q: [0] * n for q, n in self.NDMA.items()}
        self.drr = {q: 0 for q in self.NDMA}
        self.res = {}
        self.n_ops = 0

    def R(self, key):
        r = self.res.get(key)
        if r is None:
            r = self.res[key] = Res()
        return r

    @staticmethod
    def _flat(xs):
        out = []
        for x in xs:
            if isinstance(x, (list, tuple)):
                out.extend(x)
            else:
                out.append(x)
        return out

    def _deps(self, reads, writes):
        deps = []
        for r in reads:
            if r.w is not None:
                deps.append(r.w)
        for w in writes:
            if w.w is not None:
                deps.append(w.w)
            deps.extend(w.r)
        return deps

    def _waits(self, eng, deps):
        out = {}
        wd = self.waited[eng]
        for (k, v) in deps:
            if k == "pe" and eng == "pe":
                continue
            if k == eng and v > self.cnt[eng]:
                continue
            if wd.get(k, 0) >= v:
                continue
            if out.get(k, 0) < v:
                out[k] = v
        for k, v in out.items():
            wd[k] = v
        return list(out.items())

    def op(self, eng, fn, reads=(), writes=(), inc=True, defer=False):
        self.n_ops += 1
        reads = self._flat(reads)
        writes = self._flat(writes)
        deps = self._deps(reads, writes)
        waits = self._waits(eng, deps)
        if not inc:
            self.ops[eng].append((waits, fn, None))
            return None
        if defer:
            ev = (eng, self.cnt[eng] + 1)
            self.ops[eng].append((waits, fn, None))
        else:
            self.cnt[eng] += 1
            ev = (eng, self.cnt[eng])
            self.ops[eng].append((waits, fn, ev))
        for w in writes:
            w.w = ev
            w.r = []
        for r in reads:
            r.r.append(ev)
        return ev

    def dma(self, q, fn, reads=(), writes=()):
        self.n_ops += 1
        reads = self._flat(reads)
        writes = self._flat(writes)
        n = self.NDMA[q]
        i = self.drr[q] % n
        self.drr[q] += 1
        s = self.dsem[q][i]
        prev = self.dcnt[q][i]
        deps = self._deps(reads, writes)
        if prev > 0:
            deps.append((s, prev))
        waits = self._waits(q, deps)
        self.dcnt[q][i] = prev + 16
        ev = (s, prev + 16)
        self.ops[q].append((waits, fn, ev))
        for w in writes:
            w.w = ev
            w.r = []
        for r in reads:
            r.r.append(ev)
        return ev

    def all_events(self):
        evs = []
        for e in self.ENGS:
            if self.cnt[e] > 0:
                evs.append((e, self.cnt[e]))
        for q in self.NDMA:
            for s, c in zip(self.dsem[q], self.dcnt[q]):
                if c > 0:
                    evs.append((s, c))
        return evs

    def barrier(self):
        evs = self.all_events()
        for e in self.ENGS:
            waits = self._waits(e, evs)
            if waits:
                self.ops[e].append((waits, None, None))
        for r in self.res.values():
            r.w = None
            r.r = []

    def emit(self, block):
        self.barrier()
        needed = {e: set() for e in self.ENGS}
        for e in self.ENGS:
            for waits, fn, own in self.ops[e]:
                for k, v in waits:
                    if isinstance(k, str):
                        needed[k].add(v)
        cmap = {}
        for e in self.ENGS:
            cmap[e] = {v: i + 1 for i, v in enumerate(sorted(needed[e]))}
        self.n_incs = {e: len(needed[e]) for e in self.ENGS}

        def run(e, ename):
            for waits, fn, own in self.ops[ename]:
                for k, v in waits:
                    if isinstance(k, str):
                        e.wait_ge(self.sem[k], cmap[k][v])
                    else:
                        e.wait_ge(k, v)
                if fn is None:
                    continue
                ins = fn(e)
                if own is not None:
                    if isinstance(own[0], str):
                        if own[1] in needed[ename]:
                            ins.then_inc(self.sem[ename], 1)
                    else:
                        ins.then_inc(own[0], own[1] if False else 16)

        block.tensor(lambda e: run(e, "pe"))
        block.scalar(lambda e: run(e, "act"))
        block.vector(lambda e: run(e, "dve"))
        block.gpsimd(lambda e: run(e, "pool"))
        block.sync(lambda e: run(e, "sp"))


class Mem:
    def __init__(self, nc, limit=229000):
        self.nc = nc
        self.top = 16640
        self.n = 0
        self.limit = limit
        self.peak = 0

    def alloc(self, shape, dtype):
        nb = 2 if dtype == BF16 else 4
        n = 1
        for s in shape[1:]:
            n *= s
        off = (self.top + 63) // 64 * 64
        self.top = off + n * nb
        self.peak = max(self.peak, self.top)
        assert self.top <= self.limit, ("SBUF overflow", self.top)
        self.n += 1
        return self.nc.alloc_sbuf_tensor_at("t%d" % self.n, list(shape), dtype, offset=off)

    def mark(self):
        return self.top

    def release(self, m):
        self.top = m


def build_program(NSEQ=2, debug=False, layers=(0, 1), phases=("P1", "P2", "KV", "A", "B", "C", "OUT", "FIN"), heads=tuple(range(12))):
    nc = bass.Bass("TRN2", target_bir_lowering=False)

    def din(name, shape):
        return nc.dram_tensor(name, list(shape), F32, kind="ExternalInput").ap()

    x_in = din("x", [NSEQ, SEQ, D_MODEL])
    mem_in = din("mem", [NSEQ, 256, D_MODEL])
    norm_g = din("norm_g", [2, D_MODEL])
    big = any(p in phases for p in ("P2", "OUT", "KV"))
    w_in = din("w_in", [2, D_MODEL, N_IN] if big else [2, 8, 8])
    sgu_ln_g = din("sgu_ln_g", [2, 1536])
    sgu_ln_b = din("sgu_ln_b", [2, 1536])
    sgu_w = din("sgu_w", [2, 12, 128, 128])
    sgu_b = din("sgu_b", [2, 12, 128])
    conv_w = din("conv_w", [2, 5, 4608])
    a_log = din("a_log", [2, 2, 12])
    dt_bias = din("dt_bias", [2, 2, 12])
    gdn_norm_g = din("gdn_norm_g", [2, 128])
    mem_norm_g = din("mem_norm_g", [2, D_MODEL])
    w_mem_kv = din("w_mem_kv", [2, D_MODEL, 2048] if big else [2, 8, 8])
    w_out = din("w_out", [2, D_MODEL, D_MODEL] if big else [2, 8, 8])
    final_g = din("final_g", [D_MODEL])
    out = nc.dram_tensor("out", [NSEQ, SEQ, D_MODEL], F32, kind="ExternalOutput").ap()
    skind = "ExternalOutput" if debug else "Internal"
    zT = nc.dram_tensor("zT", [N_IN, SEQ], F32, kind=skind).ap()
    va_tok = nc.dram_tensor("va_tok", [SEQ, 1536], F32, kind=skind).ap()
    yT = nc.dram_tensor("yT", [D_MODEL, SEQ], BF16, kind=skind).ap()
    x1 = nc.dram_tensor("x1", [NSEQ, SEQ, D_MODEL], F32, kind=skind).ap()

    S = Sched(nc)
    M = Mem(nc)
    R = S.R
    PS = [nc.alloc_psum_tensor("psb%d" % i, [128, 512], F32) for i in range(8)]
    RP = [[R("ps%da" % i), R("ps%db" % i)] for i in range(8)]

    def RH(i, half):
        return RP[i]

    def ph(i, half):
        return PS[i][:, half * 256:(half + 1) * 256].rearrange("p (a b) -> p a b", b=128)

    def phb(i, half):
        return PS[i][:, half * 256:(half + 1) * 256].bitcast(BF16).rearrange("p (a b) -> p a b", b=128)

    dbg_seen = set()

    def dbg(name, t, shape, dtype, res):
        if not debug or ("dbg_" + name) in dbg_seen:
            return
        dbg_seen.add("dbg_" + name)
        dt_ = nc.dram_tensor("dbg_" + name, list(shape), dtype, kind="ExternalOutput").ap()
        S.dma("sp", lambda e: e.dma_start(out=dt_, in_=t[:]), reads=[res], writes=[R("dbg_" + name)])

    def psb(i):
        return PS[i][:].bitcast(BF16).rearrange("p (a b) -> p a b", b=128)

    def psf(i):
        return PS[i][:].rearrange("p (a b) -> p a b", b=128)

    ident_f = M.alloc([128, 128], F32)
    ident_b = M.alloc([128, 128], BF16)
    ones_f = M.alloc([128, 128], F32)
    ones_b = M.alloc([128, 128], BF16)
    Ltri = M.alloc([128, 128], F32)
    Utri = M.alloc([128, 128], F32)
    BIG = [M.alloc([128, 4, 128], F32) for _ in range(2)]
    ST01 = [M.alloc([128, 128], F32) for _ in range(2)]
    eps_t = M.alloc([128, 1], F32)
    one_t = M.alloc([128, 1], F32)
    RC = R("consts")

    def sel_fill(t_ap, val, pattern, cm, cmp):
        S.op("pool", lambda e: e.memset(t_ap, val), writes=[RC])
        S.op("pool", lambda e: e.affine_select(out=t_ap, in_=t_ap, pattern=pattern, compare_op=cmp, fill=0.0, base=0, channel_multiplier=cm), reads=[RC], writes=[RC])

    sel_fill(ident_f[:], 1.0, [[-1, 128]], 1, ALU.is_equal)
    S.op("pool", lambda e: e.memset(ones_f[:], 1.0), writes=[RC])
    S.op("pool", lambda e: e.memset(ones_b[:], 1.0), writes=[RC])
    S.op("pool", lambda e: e.memset(eps_t[:], EPS), writes=[RC])
    S.op("pool", lambda e: e.memset(one_t[:], 1.0), writes=[RC])
    S.op("pool", lambda e: e.tensor_copy(out=ident_b[:], in_=ident_f[:]), reads=[RC], writes=[RC])
    sel_fill(Ltri[:], 1.0, [[1, 128]], -1, ALU.is_ge)
    sel_fill(Utri[:], 1.0, [[-1, 128]], 1, ALU.is_ge)
    sel_fill(BIG[0][:], BIGM, [[0, 4], [1, 128]], -1, ALU.is_gt)
    sel_fill(BIG[1][:], BIGM, [[0, 4], [-1, 128]], 1, ALU.is_gt)
    sel_fill(ST01[0][:], 1.0, [[-1, 128]], 1, ALU.is_gt)
    sel_fill(ST01[1][:], 1.0, [[1, 128]], -1, ALU.is_gt)

    ng_col = M.alloc([128, 32], F32)
    mg_col = M.alloc([128, 32], F32)
    lng_col = M.alloc([128, 12], F32)
    lnb_col = M.alloc([128, 12], F32)
    cw_col = M.alloc([128, 36, 5], F32)
    gn_col = M.alloc([128, 1], F32)
    wsT = M.alloc([128, 12, 128], BF16)
    Rsg = M.alloc([128, 12, 128], F32)
    negA = M.alloc([128, 24], F32)
    dtb = M.alloc([128, 24], F32)
    bd_tok = M.alloc([128, NT, 48], F32)
    RPAR = R("params")
    RTAB = R("tables")
    base_mark = M.mark()

    def load_layer_params(l):
        mk = M.mark()
        st = M.alloc([36, 128], F32)
        st2 = M.alloc([128, 12, 128], F32)
        st3 = M.alloc([5, 4608], F32)
        bbc = M.alloc([128, 12, 128], F32)
        alb = M.alloc([128, 24], F32)
        Rst = R("pstage")

        def rows_to_cols(src_rows_ap, nrows, dst_ap):
            S.dma("sp", lambda e: e.dma_start(out=st[0:nrows, :], in_=src_rows_ap), writes=[Rst])
            S.op("pe", lambda e: e.matmul(PS[0][:, 0:nrows], st[0:nrows, :], ident_f[0:nrows, 0:nrows], start=True, stop=True), reads=[Rst, RC], writes=[RP[0]])
            S.op("dve", lambda e: e.tensor_copy(out=dst_ap, in_=PS[0][:, 0:nrows]), reads=[RP[0]], writes=[RPAR])

        rows_to_cols(norm_g[l].rearrange("(k p) -> k p", p=128), 32, ng_col[:])
        rows_to_cols(mem_norm_g[l].rearrange("(k p) -> k p", p=128), 32, mg_col[:])
        rows_to_cols(sgu_ln_g[l].rearrange("(k p) -> k p", p=128), 12, lng_col[:])
        rows_to_cols(sgu_ln_b[l].rearrange("(k p) -> k p", p=128), 12, lnb_col[:])
        rows_to_cols(gdn_norm_g[l].rearrange("(k p) -> k p", p=128), 1, gn_col[:])
        S.dma("sp", lambda e: e.dma_start(out=st3[:], in_=conv_w[l]), writes=[Rst])
        for t in range(36):
            S.op("pe", lambda e, t=t: e.matmul(PS[1][:, t * 5:(t + 1) * 5], st3[0:5, t * 128:(t + 1) * 128], ident_f[0:5, 0:5], start=True, stop=True),
                 reads=[Rst, RC], writes=[RP[1]], inc=(t == 35))
        S.op("dve", lambda e: e.tensor_copy(out=cw_col[:].rearrange("p a b -> p (a b)"), in_=PS[1][:, 0:180]), reads=[RP[1]], writes=[RPAR])
        S.dma("sp", lambda e: e.dma_start(out=st2[:], in_=sgu_w[l].rearrange("g t s -> t g s")), writes=[Rst])
        for g in range(12):
            S.op("pe", lambda e, g=g: e.matmul(PS[2 + g // 4][:, (g % 4) * 128:(g % 4 + 1) * 128], st2[:, g, :], ident_f[:], start=True, stop=True),
                 reads=[Rst, RC], writes=[RP[2 + g // 4]])
        for b in range(3):
            S.op("dve", lambda e, b=b: e.tensor_copy(out=wsT[:, b * 4:(b + 1) * 4, :], in_=psf(2 + b)), reads=[RP[2 + b]], writes=[RPAR])
        S.dma("sp", lambda e: e.dma_start(out=bbc[:].rearrange("p a b -> p (a b)"), in_=sgu_b[l].rearrange("g t -> (g t)").partition_broadcast(128)), writes=[Rst])
        for g in range(12):
            S.op("pe", lambda e, g=g: e.matmul(PS[5 + g // 4][:, (g % 4) * 128:(g % 4 + 1) * 128], ones_b[:], wsT[:, g, :], start=True, stop=True),
                 reads=[RPAR, RC], writes=[RP[5 + g // 4]])
        for g in range(12):
            S.op("dve", lambda e, g=g: e.scalar_tensor_tensor(out=Rsg[:, g, :], in0=PS[5 + g // 4][:, (g % 4) * 128:(g % 4 + 1) * 128], scalar=lnb_col[:, g:g + 1], in1=bbc[:, g, :], op0=ALU.mult, op1=ALU.add),
                 reads=[RP[5 + g // 4], RPAR, Rst], writes=[RPAR])
        S.dma("sp", lambda e: e.dma_start(out=alb[:], in_=a_log[l].rearrange("d h -> (d h)").partition_broadcast(128)), writes=[Rst])
        S.dma("sp", lambda e: e.dma_start(out=dtb[:], in_=dt_bias[l].rearrange("d h -> (d h)").partition_broadcast(128)), writes=[RPAR])
        S.op("act", lambda e: e.activation(out=negA[:], in_=alb[:], func=AF.Exp), reads=[Rst], writes=[RPAR])
        S.op("dve", lambda e: e.tensor_scalar(out=negA[:], in0=negA[:], scalar1=-1.0, scalar2=0.0, op0=ALU.mult, op1=ALU.add), reads=[RPAR], writes=[RPAR])
        S.barrier()
        M.release(mk)

    def norm_transpose(src_fn, ntiles, gcol, dstT, tok0, rdst):
        mk = M.mark()
        xb = [M.alloc([128, D_MODEL], F32) for _ in range(2)]
        xs1 = M.alloc([128, D_MODEL], BF16)
        xs = [xs1, xs1]
        ss = [M.alloc([128, 1], F32) for _ in range(2)]
        rs = [M.alloc([128, 1], F32) for _ in range(2)]
        for t in range(ntiles):
            i = t % 2
            rx, rxs, rss, rrs = R("nx%d" % i), R("nxs"), R("nss%d" % i), R("nrs%d" % i)
            S.dma("sp", lambda e, t=t, i=i: e.dma_start(out=xb[i][:], in_=src_fn(t)), writes=[rx])
            S.op("pool", lambda e, i=i: e.memset(ss[i][:], 0.0), writes=[rss])
            S.op("act", lambda e, i=i: e.activation(out=xs[i][:], in_=xb[i][:], func=AF.Square, accum_out=ss[i][:]), reads=[rx], writes=[rxs, rss])
            S.op("act", lambda e, i=i: e.activation(out=rs[i][:], in_=ss[i][:], func=AF.Sqrt, bias=eps_t[:], scale=1.0 / D_MODEL), reads=[rss, RC], writes=[rrs])
            S.op("dve", lambda e, i=i: e.reciprocal(out=rs[i][:], in_=rs[i][:]), reads=[rrs], writes=[rrs])
            S.op("act", lambda e, i=i: e.activation(out=xs[i][:], in_=xb[i][:], func=AF.Identity, scale=rs[i][:]), reads=[rx, rrs], writes=[rxs])
            for kb in range(4):
                b = kb % 2
                for kk in range(8):
                    k = kb * 8 + kk
                    S.op("pe", lambda e, i=i, k=k, kk=kk, b=b: e.transpose(psb(b)[:, kk, :], xs[i][:, k * 128:(k + 1) * 128], ident_b[:]),
                         reads=[rxs, RC], writes=[RP[b]], inc=(kk == 7))
                S.op("dve", lambda e, kb=kb, b=b, t=t: e.tensor_tensor(out=dstT[:, kb * 8:(kb + 1) * 8, tok0 + t * 128:tok0 + (t + 1) * 128], in0=psb(b),
                                                                      in1=gcol[:, kb * 8:(kb + 1) * 8].unsqueeze(2).to_broadcast([128, 8, 128]), op=ALU.mult),
                     reads=[RP[b], RPAR], writes=[rdst])
        S.barrier()
        M.release(mk)

    wbufs = None

    def load_wblock(wsrc, c0, ncols, i):
        v = wsrc.rearrange("(k p) c -> p k c", p=128)
        dst = wbufs[i]
        for hh in range(2):
            S.dma("pool", lambda e, hh=hh, dst=dst: e.dma_start(out=dst[:, hh * 16:(hh + 1) * 16, 0:ncols], in_=v[:, hh * 16:(hh + 1) * 16, c0:c0 + ncols]), writes=[R("wb%d" % i)])

    def in_proj(l, xsrc):
        nonlocal wbufs
        mk = M.mark()
        hT = M.alloc([128, 32, 1024], BF16)
        wbufs = [M.alloc([128, 32, 256], BF16) for _ in range(2)]
        zst = [M.alloc([128, 1024], F32) for _ in range(2)]
        vst = [M.alloc([128, 256], F32) for _ in range(2)]
        rh = R("hT")
        blocks = []
        for c0 in range(0, 1536, 256):
            blocks.append(("fm", c0, 256))
        for c0 in range(1536, 3072, 256):
            blocks.append(("tm", c0, 256))
        for c0 in range(3072, 10752, 256):
            blocks.append(("fm", c0, 256))
        blocks.append(("bd", 10752, 48))
        for c0 in range(10800, N_IN, 256):
            blocks.append(("fm", c0, 256))
        for tb in range(2):
            norm_transpose(lambda t, tb=tb: xsrc[tb * 1024 + t * 128: tb * 1024 + (t + 1) * 128, :], 8, ng_col, hT, 0, rh)
            zi = 0
            vi = 0
            for bi, (kind, c0, ncols) in enumerate(blocks):
                wi = bi % 2
                rw = R("wb%d" % wi)
                load_wblock(w_in[l], c0, ncols, wi)
                if kind == "fm":
                    for c in range(2):
                        pb = (zi % 2) * 2
                        for k in range(32):
                            for hf in range(2):
                                S.op("pe", lambda e, wb=wbufs[wi], c=c, k=k, hf=hf, pb=pb: e.matmul(PS[pb + hf][:], wb[:, k, c * 128:(c + 1) * 128], hT[:, k, hf * 512:(hf + 1) * 512], start=(k == 0), stop=(k == 31)),
                                     reads=[rw, rh], writes=[RP[pb + hf]], inc=(k == 31))
                        zs = zi % 2
                        rz = R("zst%d" % zs)
                        S.op("act", lambda e, zs=zs, pb=pb: e.activation(out=zst[zs][:, 0:512], in_=PS[pb][:], func=AF.Copy), reads=[RP[pb]], writes=[rz])
                        S.op("dve", lambda e, zs=zs, pb=pb: e.tensor_copy(out=zst[zs][:, 512:1024], in_=PS[pb + 1][:]), reads=[RP[pb + 1]], writes=[rz])
                        r0 = c0 + c * 128
                        S.dma("sp", lambda e, zs=zs, r0=r0, tb=tb: e.dma_start(out=zT[r0:r0 + 128, tb * 1024:(tb + 1) * 1024], in_=zst[zs][:]), reads=[rz], writes=[R("zT")])
                        zi += 1
                elif kind == "tm":
                    for tt in range(8):
                        pb = 4 + (vi % 2)
                        for k in range(32):
                            S.op("pe", lambda e, wb=wbufs[wi], k=k, tt=tt, pb=pb: e.matmul(PS[pb][:, 0:256], hT[:, k, tt * 128:(tt + 1) * 128], wb[:, k, 0:256], start=(k == 0), stop=(k == 31)),
                                 reads=[rw, rh], writes=[RP[pb]], inc=(k == 31))
                        vs = vi % 2
                        rv = R("vst%d" % vs)
                        if vi % 2 == 0:
                            S.op("act", lambda e, vs=vs, pb=pb: e.activation(out=vst[vs][:], in_=PS[pb][:, 0:256], func=AF.Copy), reads=[RP[pb]], writes=[rv])
                        else:
                            S.op("dve", lambda e, vs=vs, pb=pb: e.tensor_copy(out=vst[vs][:], in_=PS[pb][:, 0:256]), reads=[RP[pb]], writes=[rv])
                        t0 = tb * 1024 + tt * 128
                        S.dma("sp", lambda e, vs=vs, t0=t0, c0=c0: e.dma_start(out=va_tok[t0:t0 + 128, c0 - 1536:c0 - 1536 + 256], in_=vst[vs][:]), reads=[rv], writes=[R("va_tok")])
                        vi += 1
                else:
                    for tt in range(8):
                        for k in range(32):
                            S.op("pe", lambda e, wb=wbufs[wi], k=k, tt=tt: e.matmul(PS[6][:, tt * 48:(tt + 1) * 48], hT[:, k, tt * 128:(tt + 1) * 128], wb[:, k, 0:48], start=(k == 0), stop=(k == 31)),
                                 reads=[rw, rh], writes=[RP[6]], inc=(k == 31 and tt == 7))
                    S.op("dve", lambda e, tb=tb: e.tensor_copy(out=bd_tok[:, tb * 8:(tb + 1) * 8, :].rearrange("p a b -> p (a b)"), in_=PS[6][:, 0:384]), reads=[RP[6]], writes=[RTAB])
        S.barrier()
        M.release(mk)

    def out_proj(l, xsrc, xdst):
        nonlocal wbufs
        mk = M.mark()
        yTb = M.alloc([128, 32, 1024], BF16)
        wbufs = [M.alloc([128, 32, 512], BF16) for _ in range(2)]
        xr = [M.alloc([128, 512], F32) for _ in range(4)]
        ry = R("yTb")
        yv = yT.rearrange("(k p) t -> p k t", p=128)
        it = 0
        for tb in range(2):
            for q4 in range(4):
                S.dma("sp", lambda e, tb=tb, q4=q4: e.dma_start(out=yTb[:, q4 * 8:(q4 + 1) * 8, :], in_=yv[:, q4 * 8:(q4 + 1) * 8, tb * 1024:(tb + 1) * 1024]), reads=[R("yT")], writes=[ry])
            for cb in range(8):
                wi = cb % 2
                rw = R("wb%d" % wi)
                load_wblock(w_out[l], cb * 512, 512, wi)
                for tt in range(8):
                    pb = it % 4
                    xi = it % 4
                    rx = R("xr%d" % xi)
                    t0 = tb * 1024 + tt * 128
                    S.dma("sp", lambda e, xi=xi, t0=t0, cb=cb: e.dma_start(out=xr[xi][:], in_=xsrc[t0:t0 + 128, cb * 512:(cb + 1) * 512]), writes=[rx])
                    for k in range(32):
                        S.op("pe", lambda e, wb=wbufs[wi], k=k, tt=tt, pb=pb: e.matmul(PS[pb][:], yTb[:, k, tt * 128:(tt + 1) * 128], wb[:, k, :], start=(k == 0), stop=(k == 31)),
                             reads=[rw, ry], writes=[RP[pb]], inc=(k == 31))
                    S.op("dve", lambda e, xi=xi, pb=pb: e.tensor_tensor(out=xr[xi][:], in0=PS[pb][:], in1=xr[xi][:], op=ALU.add), reads=[RP[pb], rx], writes=[rx])
                    S.dma("sp", lambda e, xi=xi, t0=t0, cb=cb: e.dma_start(out=xdst[t0:t0 + 128, cb * 512:(cb + 1) * 512], in_=xr[xi][:]), reads=[rx], writes=[R("xdst")])
                    it += 1
        S.barrier()
        M.release(mk)

    def final_norm(s):
        mk = M.mark()
        fg = M.alloc([128, D_MODEL], F32)
        xb = [M.alloc([128, D_MODEL], F32) for _ in range(2)]
        junk = M.alloc([128, D_MODEL], BF16)
        ss = [M.alloc([128, 1], F32) for _ in range(2)]
        S.dma("sp", lambda e: e.dma_start(out=fg[:], in_=final_g.partition_broadcast(128)), writes=[R("fg")])
        for t in range(NT):
            i = t % 2
            rx, rss = R("fx%d" % i), R("fss%d" % i)
            S.dma("sp", lambda e, t=t, i=i: e.dma_start(out=xb[i][:], in_=out[s, t * 128:(t + 1) * 128, :]), reads=[R("xdst")], writes=[rx])
            S.op("pool", lambda e, i=i: e.memset(ss[i][:], 0.0), writes=[rss])
            S.op("act", lambda e, i=i: e.activation(out=junk[:], in_=xb[i][:], func=AF.Square, accum_out=ss[i][:]), reads=[rx], writes=[R("fjunk"), rss])
            S.op("act", lambda e, i=i: e.activation(out=ss[i][:], in_=ss[i][:], func=AF.Sqrt, bias=eps_t[:], scale=1.0 / D_MODEL), reads=[rss, RC], writes=[rss])
            S.op("dve", lambda e, i=i: e.reciprocal(out=ss[i][:], in_=ss[i][:]), reads=[rss], writes=[rss])
            S.op("dve", lambda e, i=i: e.scalar_tensor_tensor(out=xb[i][:], in0=xb[i][:], scalar=ss[i][:], in1=fg[:], op0=ALU.mult, op1=ALU.mult), reads=[rx, rss, R("fg")], writes=[rx])
            S.dma("sp", lambda e, t=t, i=i: e.dma_start(out=out[s, t * 128:(t + 1) * 128, :], in_=xb[i][:]), reads=[rx], writes=[R("outfin")])
        S.barrier()
        M.release(mk)

    def branch_kvc(l, s):
        nonlocal wbufs
        mk = M.mark()
        mT = M.alloc([128, 32, 256], BF16)
        wbufs = [M.alloc([128, 32, 256], BF16) for _ in range(2)]
        kT = M.alloc([128, 8, 256], BF16)
        vtok = M.alloc([128, 2, 1024], BF16)
        rm, rk, rv = R("mT"), R("kT"), R("vtok")
        norm_transpose(lambda t: mem_in[s, t * 128:(t + 1) * 128, :], 2, mg_col, mT, 0, rm)
        it = 0
        for blk in range(8):
            wi = blk % 2
            rw = R("wb%d" % wi)
            load_wblock(w_mem_kv[l], blk * 256, 256, wi)
            for c in range(2):
                pb = it % 2
                it += 1
                if blk < 4:
                    tile = blk * 2 + c
                    for k in range(32):
                        S.op("pe", lambda e, wb=wbufs[wi], k=k, c=c, pb=pb: e.matmul(PS[pb][:, 0:256], wb[:, k, c * 128:(c + 1) * 128], mT[:, k, :], start=(k == 0), stop=(k == 31)),
                             reads=[rw, rm], writes=[RP[pb]], inc=(k == 31))
                    S.op("act", lambda e, pb=pb, tile=tile: e.activation(out=kT[:, tile, :], in_=PS[pb][:, 0:256], func=AF.Copy), reads=[RP[pb]], writes=[rk])
                else:
                    mt = c
                    for k in range(32):
                        S.op("pe", lambda e, wb=wbufs[wi], k=k, mt=mt, pb=pb: e.matmul(PS[pb][:, 0:256], mT[:, k, mt * 128:(mt + 1) * 128], wb[:, k, :], start=(k == 0), stop=(k == 31)),
                             reads=[rw, rm], writes=[RP[pb]], inc=(k == 31))
                    S.op("dve", lambda e, pb=pb, mt=mt, blk=blk: e.tensor_copy(out=vtok[:, mt, (blk - 4) * 256:(blk - 3) * 256], in_=PS[pb][:, 0:256]), reads=[RP[pb]], writes=[rv])
        dbg("kT", kT, [128, 8, 256], BF16, rk)
        dbg("vtok", vtok, [128, 2, 1024], BF16, rv)
        dbg("mT", mT, [128, 32, 256], BF16, rm)
        if "C" in phases:
            qc = [M.alloc([128, 8, 512], BF16) for _ in range(2)]
            gc = [M.alloc([128, 8, 512], F32) for _ in range(2)]
            eT = [M.alloc([128, 2, 512], BF16) for _ in range(2)]
            rden = [M.alloc([128, 512], F32) for _ in range(2)]
            tmp = [M.alloc([128, 512], F32) for _ in range(2)]
            yc = [M.alloc([128, 2, 512], BF16) for _ in range(2)]
            ih = 0
            for tb4 in range(4):
                i = tb4 % 2
                rq, rg = R("qc%d" % i), R("gcf%d" % i)
                t0 = tb4 * 512
                S.dma("pool", lambda e, i=i, t0=t0: e.dma_start(out=qc[i][:], in_=zT[10800:11824, t0:t0 + 512].rearrange("(j p) t -> p j t", p=128)), reads=[R("zT")], writes=[rq])
                S.dma("sp", lambda e, i=i, t0=t0: e.dma_start(out=gc[i][:], in_=zT[11824:12848, t0:t0 + 512].rearrange("(j p) t -> p j t", p=128)), reads=[R("zT")], writes=[rg])
                S.op("act", lambda e, i=i: e.activation(out=gc[i][:], in_=gc[i][:], func=AF.Silu), reads=[rg], writes=[rg])
                for h in range(4):
                    j = ih % 2
                    ih += 1
                    re_, rd, rt, ry = R("eT%d" % j), R("rden%d" % j), R("ctmp%d" % j), R("yc%d" % j)
                    for mt in range(2):
                        for dt in range(2):
                            S.op("pe", lambda e, h=h, dt=dt, mt=mt, i=i: e.matmul(PS[2 + mt][:], kT[:, h * 2 + dt, mt * 128:(mt + 1) * 128], qc[i][:, h * 2 + dt, :], start=(dt == 0), stop=(dt == 1)),
                                 reads=[rk, rq], writes=[RP[2 + mt]], inc=(dt == 1))
                        S.op("act", lambda e, j=j, mt=mt: e.activation(out=eT[j][:, mt, :], in_=PS[2 + mt][:], func=AF.Exp, scale=1.0 / 16.0), reads=[RP[2 + mt]], writes=[re_])
                    for mt in range(2):
                        S.op("pe", lambda e, j=j, mt=mt: e.matmul(PS[4][:], ones_b[:], eT[j][:, mt, :], start=(mt == 0), stop=(mt == 1)), reads=[re_, RC], writes=[RP[4]], inc=(mt == 1))
                    S.op("dve", lambda e, j=j: e.reciprocal(out=rden[j][:], in_=PS[4][:]), reads=[RP[4]], writes=[rd])
                    if tb4 == 0 and h == 0:
                        dbg("eT", eT[j], [128, 2, 512], BF16, re_)
                        dbg("rden", rden[j], [128, 512], F32, rd)
                        dbg("qc", qc[i], [128, 8, 512], BF16, rq)
                    for dt in range(2):
                        for mt in range(2):
                            S.op("pe", lambda e, j=j, h=h, dt=dt, mt=mt: e.matmul(PS[5 + dt][:], vtok[:, mt, h * 256 + dt * 128:h * 256 + (dt + 1) * 128], eT[j][:, mt, :], start=(mt == 0), stop=(mt == 1)),
                                 reads=[rv, re_], writes=[RP[5 + dt]], inc=(mt == 1))
                        S.op("dve", lambda e, j=j, dt=dt: e.tensor_tensor(out=tmp[j][:], in0=PS[5 + dt][:], in1=rden[j][:], op=ALU.mult), reads=[RP[5 + dt], rd], writes=[rt])
                        S.op("pool", lambda e, j=j, dt=dt, i=i, h=h: e.tensor_tensor(out=yc[j][:, dt, :], in0=tmp[j][:], in1=gc[i][:, h * 2 + dt, :], op=ALU.mult), reads=[rt, rg], writes=[ry])
                    S.dma("sp", lambda e, j=j, h=h, t0=t0: e.dma_start(out=yT[3072 + h * 256:3072 + (h + 1) * 256, t0:t0 + 512].rearrange("(dt p) t -> p dt t", p=128), in_=yc[j][:]), reads=[ry], writes=[R("yT")])
        S.barrier()
        M.release(mk)

    def branch_a(l, s):
        mk = M.mark()
        va = [M.alloc([128, 1536], F32) for _ in range(2)]
        uT = [M.alloc([128, 12, 128], F32) for _ in range(2)]
        ga = [M.alloc([128, 12, 128], F32) for _ in range(2)]
        nrm = [M.alloc([128, 1536], BF16) for _ in range(2)]
        mixed = [M.alloc([128, 12, 128], F32) for _ in range(2)]
        ya = [M.alloc([128, 12, 128], BF16) for _ in range(2)]
        st = [M.alloc([128, 8], F32) for _ in range(2)]
        for n in range(NT):
            i = n % 2
            rva, ru, rg, rn, rmx, rya, rst = R("va%d" % i), R("uT%d" % i), R("ga%d" % i), R("nrm%d" % i), R("mixed%d" % i), R("ya%d" % i), R("ast%d" % i)
            t0 = n * 128
            S.dma("sp", lambda e, i=i, t0=t0: e.dma_start(out=va[i][:], in_=va_tok[t0:t0 + 128, :]), reads=[R("va_tok")], writes=[rva])
            S.dma("sp", lambda e, i=i, t0=t0: e.dma_start(out=uT[i][:], in_=zT[0:1536, t0:t0 + 128].rearrange("(g c) t -> c g t", c=128)), reads=[R("zT")], writes=[ru])
            S.dma("sp", lambda e, i=i, t0=t0: e.dma_start(out=ga[i][:], in_=zT[3072:4608, t0:t0 + 128].rearrange("(g c) t -> c g t", c=128)), reads=[R("zT")], writes=[rg])
            sm = st[i]
            S.op("pool", lambda e, sm=sm: e.memset(sm[:], 0.0), writes=[rst])
            S.op("act", lambda e, i=i, sm=sm: e.activation(out=va[i][:], in_=va[i][:], func=AF.Gelu_apprx_tanh, accum_out=sm[:, 0:1]), reads=[rva, rst], writes=[rva, rst])
            S.op("act", lambda e, i=i, sm=sm: e.activation(out=nrm[i][:], in_=va[i][:], func=AF.Square, accum_out=sm[:, 1:2]), reads=[rva, rst], writes=[rn, rst])
            S.op("dve", lambda e, sm=sm: e.tensor_scalar(out=sm[:, 2:3], in0=sm[:, 0:1], scalar1=1.0 / 1536, scalar2=0.0, op0=ALU.mult, op1=ALU.add), reads=[rst], writes=[rst])
            S.op("dve", lambda e, sm=sm: e.tensor_tensor(out=sm[:, 3:4], in0=sm[:, 2:3], in1=sm[:, 2:3], op=ALU.mult), reads=[rst], writes=[rst])
            S.op("dve", lambda e, sm=sm: e.scalar_tensor_tensor(out=sm[:, 4:5], in0=sm[:, 1:2], scalar=1.0 / 1536, in1=sm[:, 3:4], op0=ALU.mult, op1=ALU.subtract), reads=[rst], writes=[rst])
            S.op("act", lambda e, sm=sm: e.activation(out=sm[:, 5:6], in_=sm[:, 4:5], func=AF.Sqrt, bias=eps_t[:], scale=1.0), reads=[rst, RC], writes=[rst])
            S.op("dve", lambda e, sm=sm: e.reciprocal(out=sm[:, 6:7], in_=sm[:, 5:6]), reads=[rst], writes=[rst])
            S.op("dve", lambda e, i=i, sm=sm: e.tensor_scalar(out=nrm[i][:], in0=va[i][:], scalar1=sm[:, 2:3], scalar2=sm[:, 6:7], op0=ALU.subtract, op1=ALU.mult), reads=[rva, rst], writes=[rn])
            pb0 = (n % 2) * 3
            for g in range(12):
                pb = pb0 + g // 4
                S.op("pe", lambda e, i=i, g=g, pb=pb: e.matmul(PS[pb][:, (g % 4) * 128:(g % 4 + 1) * 128], nrm[i][:, g * 128:(g + 1) * 128], wsT[:, g, :], start=True, stop=True),
                     reads=[rn, RPAR], writes=[RP[pb]], inc=(g % 4 == 3))
            for g in range(12):
                pb = pb0 + g // 4
                S.op("dve", lambda e, i=i, g=g, pb=pb: e.scalar_tensor_tensor(out=mixed[i][:, g, :], in0=PS[pb][:, (g % 4) * 128:(g % 4 + 1) * 128], scalar=lng_col[:, g:g + 1], in1=Rsg[:, g, :], op0=ALU.mult, op1=ALU.add),
                     reads=[RP[pb], RPAR], writes=[rmx])
            S.op("act", lambda e, i=i: e.activation(out=uT[i][:], in_=uT[i][:], func=AF.Gelu_apprx_tanh), reads=[ru], writes=[ru])
            S.op("act", lambda e, i=i: e.activation(out=ga[i][:], in_=ga[i][:], func=AF.Silu), reads=[rg], writes=[rg])
            S.op("pool", lambda e, i=i: e.tensor_tensor(out=mixed[i][:], in0=mixed[i][:], in1=uT[i][:], op=ALU.mult), reads=[rmx, ru], writes=[rmx])
            S.op("pool", lambda e, i=i: e.tensor_tensor(out=ya[i][:], in0=mixed[i][:], in1=ga[i][:], op=ALU.mult), reads=[rmx, rg], writes=[rya])
            S.dma("sp", lambda e, i=i, t0=t0: e.dma_start(out=yT[0:1536, t0:t0 + 128].rearrange("(g c) t -> c g t", c=128), in_=ya[i][:]), reads=[rya], writes=[R("yT")])
        S.barrier()
        M.release(mk)

    def branch_b(l, s):
        mk = M.mark()
        beta_t = M.alloc([128, 2, NT, 12], F32)
        g_t = M.alloc([128, 2, NT, 12], F32)
        gc_t = M.alloc([128, 2, NT, 12], F32)
        bw_t = M.alloc([128, 2, NT, 12], F32)
        kd_t = M.alloc([128, 2, NT, 12], F32)
        egl_t = M.alloc([128, 2, NT, 12], F32)
        gcT = [M.alloc([12, SEQ], F32) for _ in range(2)]
        egcT = [M.alloc([12, SEQ], BF16) for _ in range(2)]
        tA = M.alloc([128, NT, 24], F32)
        tB = M.alloc([128, NT, 24], F32)
        RT = R("gtab")
        flat = lambda t: t[:].rearrange("p d m h -> p (d m h)")
        pm = lambda t: t[:].rearrange("p d m h -> p m d h")
        S.op("act", lambda e: e.activation(out=tA[:], in_=bd_tok[:, :, 0:24], func=AF.Exp, scale=-1.0), reads=[RTAB], writes=[RT])
        S.op("dve", lambda e: e.tensor_scalar(out=tA[:], in0=tA[:], scalar1=1.0, scalar2=0.0, op0=ALU.add, op1=ALU.add), reads=[RT], writes=[RT])
        S.op("dve", lambda e: e.reciprocal(out=pm(beta_t), in_=tA[:].rearrange("p m (d h) -> p m d h", d=2)), reads=[RT], writes=[RT])
        S.op("dve", lambda e: e.tensor_tensor(out=tB[:], in0=bd_tok[:, :, 24:48], in1=dtb[:].unsqueeze(1).to_broadcast([128, NT, 24]), op=ALU.add), reads=[RTAB, RPAR], writes=[RT])
        S.op("act", lambda e: e.activation(out=tB[:], in_=tB[:], func=AF.Exp), reads=[RT], writes=[RT])
        S.op("act", lambda e: e.activation(out=tB[:], in_=tB[:], func=AF.Ln, bias=one_t[:], scale=1.0), reads=[RT, RC], writes=[RT])
        S.op("dve", lambda e: e.tensor_tensor(out=pm(g_t), in0=tB[:].rearrange("p m (d h) -> p m d h", d=2), in1=negA[:].rearrange("p (d h) -> p d h", d=2).unsqueeze(1).to_broadcast([128, NT, 2, 12]), op=ALU.mult), reads=[RT, RPAR], writes=[RT])
        for d in range(2):
            S.op("pe", lambda e, d=d: e.matmul(PS[0][:, d * 192:(d + 1) * 192], (Ltri if d == 0 else Utri)[:], g_t[:, d].rearrange("p m h -> p (m h)"), start=True, stop=True), reads=[RT, RC], writes=[RP[0]])
        S.op("dve", lambda e: e.tensor_copy(out=flat(gc_t), in_=PS[0][:, 0:384]), reads=[RP[0]], writes=[RT])
        S.op("pe", lambda e: e.matmul(PS[1][:, 0:384], ones_f[:], flat(g_t), start=True, stop=True), reads=[RT, RC], writes=[RP[1]])
        S.op("dve", lambda e: e.tensor_tensor(out=flat(kd_t), in0=PS[1][:, 0:384], in1=flat(gc_t), op=ALU.subtract), reads=[RP[1], RT], writes=[RT])
        S.op("dve", lambda e: e.tensor_copy(out=flat(egl_t), in_=PS[1][:, 0:384]), reads=[RP[1]], writes=[RT])
        S.op("act", lambda e: e.activation(out=flat(egl_t), in_=flat(egl_t), func=AF.Exp), reads=[RT], writes=[RT])
        S.op("act", lambda e: e.activation(out=flat(kd_t), in_=flat(kd_t), func=AF.Exp), reads=[RT], writes=[RT])
        S.op("act", lambda e: e.activation(out=flat(bw_t), in_=flat(gc_t), func=AF.Exp), reads=[RT], writes=[RT])
        S.op("dve", lambda e: e.tensor_tensor(out=flat(bw_t), in0=flat(bw_t), in1=flat(beta_t), op=ALU.mult), reads=[RT], writes=[RT])
        for d in range(2):
            for m in range(NT):
                pb = 2 + (m // 4) % 2
                S.op("pe", lambda e, d=d, m=m, pb=pb: e.matmul(PS[pb][0:12, (m % 4) * 128:(m % 4 + 1) * 128], g_t[:, d, m, :], (Ltri if d == 0 else Utri)[:], start=True, stop=True),
                     reads=[RT, RC], writes=[RP[pb]], inc=(m % 4 == 3))
                if m % 4 == 3:
                    S.op("dve", lambda e, d=d, m=m, pb=pb: e.tensor_copy(out=gcT[d][:, (m - 3) * 128:(m + 1) * 128], in_=PS[pb][0:12, :]), reads=[RP[pb]], writes=[RT])
            S.op("act", lambda e, d=d: e.activation(out=egcT[d][:], in_=gcT[d][:], func=AF.Exp), reads=[RT], writes=[RT])

        if "Bstop1" in phases:
            S.barrier()
            M.release(mk)
            return
        rawb = [M.alloc([128, SEQ + 4], BF16) for _ in range(2)]
        rawg = M.alloc([128, SEQ], F32)
        dg = M.alloc([128, 5, 128], BF16)
        acc = M.alloc([128, SEQ], F32)
        sqb = M.alloc([128, SEQ], BF16)
        qT = M.alloc([128, SEQ], BF16)
        kT = M.alloc([128, SEQ], BF16)
        vT = M.alloc([128, SEQ], BF16)
        qgT = [M.alloc([128, SEQ], BF16) for _ in range(2)]
        sgate = M.alloc([128, SEQ], BF16)
        rsd = [M.alloc([128, 512], F32) for _ in range(2)]
        kdm = [M.alloc([128, NT, 128], BF16) for _ in range(2)]
        u_ = [M.alloc([128, NT, 128], F32) for _ in range(2)]
        nwT = [M.alloc([128, NT, 128], BF16) for _ in range(2)]
        qkT = [M.alloc([128, NT, 128], BF16) for _ in range(2)]
        UB = []
        for _d in range(2):
            f = lambda: M.alloc([128, 2, 128], F32)
            fr = lambda: M.alloc([128, 2, 128], F32R)
            fb = lambda: M.alloc([128, 2, 128], BF16)
            UB.append((f(), f(), fr(), fr(), fr(), [fr(), fr()], [fr(), fr()], fb(), fb(), fb(), fb()))
        o_d = [M.alloc([128, NT, 128], F32) for _ in range(2)]
        Sst = [M.alloc([128, 128], F32) for _ in range(2)]
        Sbf = [M.alloc([128, 128], BF16) for _ in range(2)]
        vnew = [M.alloc([128, 128], BF16) for _ in range(2)]
        ssn = M.alloc([128, NT], F32)
        eps128 = M.alloc([128, 1], F32)
        ident_r = M.alloc([128, 128], F32R)
        S.op("pool", lambda e: e.tensor_copy(out=ident_r[:], in_=ident_f[:]), reads=[RC], writes=[RT])
        Sel12 = M.alloc([12, 12, 128], F32)
        Sel12b = M.alloc([12, 12, 128], BF16)
        S.op("pool", lambda e: e.memset(eps128[:], 128.0 * EPS), writes=[RT])
        S.op("pool", lambda e: e.tensor_copy(out=Sel12[:], in_=ident_f[0:12, 0:12].unsqueeze(2).to_broadcast([12, 12, 128])), reads=[RC], writes=[RT])
        S.op("pool", lambda e: e.tensor_copy(out=Sel12b[:], in_=Sel12[:]), reads=[RT], writes=[RT])
        for i in range(2):
            S.op("pool", lambda e, i=i: e.memset(rawb[i][:, 0:2], 0.0), writes=[R("rawb%d" % i)])
            S.op("pool", lambda e, i=i: e.memset(rawb[i][:, SEQ + 2:SEQ + 4], 0.0), writes=[R("rawb%d" % i)])
        Racc, Rsq, RqT, RkT, RvT, Rsg_, Rkdm = R("acc"), R("sqb"), R("qT"), R("kT"), R("vT"), R("sgate"), R("kdm")
        Rqg = [R("qgT0"), R("qgT1")]
        ri = [0]
        def prep1(h):
            for which, (row0, dstT, rdst) in enumerate(((4608, qT, RqT), (6144, kT, RkT), (7680, vT, RvT))):
                i = ri[0] % 2
                ri[0] += 1
                rr = R("rawb%d" % i)
                r0 = row0 + h * 128
                S.dma("pool", lambda e, i=i, r0=r0: e.dma_start(out=rawb[i][:, 2:SEQ + 2], in_=zT[r0:r0 + 128, :]), reads=[R("zT")], writes=[rr])
                ti = which * 12 + h
                rdg = R("dg")
                for j in range(5):
                    S.op("act", lambda e, ti=ti, j=j: e.activation(out=dg[:, j, :], in_=ident_f[:], func=AF.Identity, scale=cw_col[:, ti, j:j + 1]), reads=[RC, RPAR], writes=[rdg], defer=(j < 4))
                yield
                for c4 in range(4):
                    pb = 6 + c4 % 2
                    for j in range(5):
                        S.op("pe", lambda e, i=i, c4=c4, j=j, pb=pb: e.matmul(PS[pb][:], dg[:, j, :], rawb[i][:, c4 * 512 + j:c4 * 512 + j + 512], start=(j == 0), stop=(j == 4)),
                             reads=[rdg, rr], writes=[RP[pb]], inc=(j == 4))
                    if which == 2:
                        S.op("act", lambda e, c4=c4, pb=pb: e.activation(out=vT[:, c4 * 512:(c4 + 1) * 512], in_=PS[pb][:], func=AF.Silu), reads=[RP[pb]], writes=[RvT])
                    else:
                        S.op("act", lambda e, c4=c4, pb=pb: e.activation(out=acc[:, c4 * 512:(c4 + 1) * 512], in_=PS[pb][:], func=AF.Silu), reads=[RP[pb]], writes=[Racc])
                    yield
                if which != 2:
                    S.op("pool", lambda e: e.tensor_tensor(out=sqb[:], in0=acc[:], in1=acc[:], op=ALU.mult), reads=[Racc], writes=[Rsq])
                    yield
                    for c4 in range(4):
                        pb = 6 + c4 % 2
                        j = c4 % 2
                        rrs = R("rsd%d" % j)
                        S.op("pe", lambda e, c4=c4, pb=pb: e.matmul(PS[pb][:], ones_b[:], sqb[:, c4 * 512:(c4 + 1) * 512], start=True, stop=True), reads=[Rsq, RC], writes=[RP[pb]])
                        if which == 0:
                            S.op("act", lambda e, pb=pb, j=j: e.activation(out=rsd[j][:], in_=PS[pb][:], func=AF.Ln, bias=eps128[:], scale=128.0), reads=[RP[pb], RT], writes=[rrs])
                        else:
                            S.op("act", lambda e, pb=pb, j=j: e.activation(out=rsd[j][:], in_=PS[pb][:], func=AF.Ln, bias=eps_t[:], scale=1.0), reads=[RP[pb], RC], writes=[rrs])
                        S.op("act", lambda e, j=j: e.activation(out=rsd[j][:], in_=rsd[j][:], func=AF.Exp, scale=-0.5), reads=[rrs], writes=[rrs])
                        S.op("dve", lambda e, j=j, c4=c4, dstT=dstT: e.tensor_tensor(out=dstT[:, c4 * 512:(c4 + 1) * 512], in0=acc[:, c4 * 512:(c4 + 1) * 512], in1=rsd[j][:], op=ALU.mult), reads=[Racc, rrs], writes=[rdst])
                        yield
        def prep2(h):
            rr = R("rawg")
            r0 = 9216 + h * 128
            S.dma("sp", lambda e, r0=r0: e.dma_start(out=rawg[:], in_=zT[r0:r0 + 128, :]), reads=[R("zT")], writes=[rr])
            S.op("act", lambda e: e.activation(out=sgate[:], in_=rawg[:], func=AF.Silu), reads=[rr], writes=[Rsg_])
            yield
            for d in range(2):
                for c4 in range(4):
                    pb = 2 + c4 % 2
                    S.op("pe", lambda e, d=d, c4=c4, pb=pb, h=h: e.matmul(PS[pb][:], Sel12b[:, h, :], egcT[d][:, c4 * 512:(c4 + 1) * 512], start=True, stop=True), reads=[RT], writes=[RP[pb]])
                    S.op("dve", lambda e, d=d, c4=c4, pb=pb: e.tensor_tensor(out=qgT[d][:, c4 * 512:(c4 + 1) * 512], in0=PS[pb][:], in1=qT[:, c4 * 512:(c4 + 1) * 512], op=ALU.mult), reads=[RP[pb], RqT], writes=[Rqg[d]])
                    yield
        def phase1(h):
            def unit(d, G, m0, hb7, h=h):
                B1, B2, B3 = 1 + 3 * d, 2 + 3 * d, 3 + 3 * d
                RD, RDs, RA, RAT, RTT, Rqk, RTb, Rvb, Rkw = (R(n + str(d)) for n in ("Dp", "Ds", "Am", "ATm", "TT", "qkb", "TTb", "vbg", "kwg"))
                Dp, Ds, Am, ATm, TT, Pn, PTn, qkb, TTb, vbg, kwg = UB[d]
                trv = phb(7, hb7)
                for mi in range(2):
                    m = m0 + mi
                    S.op("act", lambda e, mi=mi, m=m: e.activation(out=vbg[:, mi, :], in_=trv[:, 2 + mi, :], func=AF.Identity, scale=beta_t[:, d, m, h:h + 1]), reads=[RH(7, hb7), RT], writes=[Rvb], defer=True)
                    S.op("act", lambda e, mi=mi, m=m: e.activation(out=kwg[:, mi, :], in_=trv[:, mi, :], func=AF.Identity, scale=bw_t[:, d, m, h:h + 1]), reads=[RH(7, hb7), RT], writes=[Rkw], defer=True)
                    S.op("act", lambda e, mi=mi, m=m: e.activation(out=kdm[d][:, m, :], in_=trv[:, mi, :], func=AF.Identity, scale=kd_t[:, d, m, h:h + 1]), reads=[RH(7, hb7), RT], writes=[Rkdm], defer=(mi == 0))
                S.op("pe", lambda e: e.matmul(PS[B2][:, 0:256], Sel12[:, h, :], gcT[d][:, m0 * 128:(m0 + 2) * 128], start=True, stop=False), reads=[RT], writes=[RH(B2, 0)], inc=False)
                S.op("pe", lambda e: e.matmul(PS[B2][:, 0:256], ident_f[:], BIG[d][:, 0:2, :].rearrange("p a b -> p (a b)"), start=False, stop=True), reads=[RC], writes=[RH(B2, 0)])
                yield
                for mi in range(2):
                    m = m0 + mi
                    S.op("act", lambda e, mi=mi, m=m: e.activation(out=Dp[:, mi, :], in_=ph(B2, 0)[:, mi, :], func=AF.Exp, bias=gc_t[:, d, m, h:h + 1], scale=-1.0), reads=[RH(B2, 0), RT], writes=[RD], defer=(mi == 0))
                yield
                S.op("pool", lambda e: e.tensor_tensor(out=Ds[:], in0=Dp[:], in1=ST01[d][:].unsqueeze(1).to_broadcast([128, 2, 128]), op=ALU.mult), reads=[RD, RC], writes=[RDs])
                yield
                for mi in range(2):
                    m = m0 + mi
                    S.op("dve", lambda e, mi=mi, m=m: e.scalar_tensor_tensor(out=Am[:, mi, :], in0=ph(0, 0)[:, mi, :], scalar=beta_t[:, d, m, h:h + 1], in1=Ds[:, mi, :], op0=ALU.mult, op1=ALU.mult), reads=[RH(0, 0), RT, RDs], writes=[RA], defer=True)
                S.op("dve", lambda e: e.tensor_tensor(out=qkb[:], in0=ph(0, 1), in1=Dp[:], op=ALU.mult), reads=[RH(0, 1), RD], writes=[Rqk])
                yield
                for mi in range(2):
                    S.op("pe", lambda e, mi=mi: e.matmul(ph(B1, 0)[:, mi, :], Am[:, mi, :], ident_r[:], start=True, stop=True), reads=[RA, RT], writes=[RH(B1, 0)], inc=(mi == 1))
                for mi in range(2):
                    S.op("pe", lambda e, mi=mi: e.transpose(phb(B3, 0)[:, mi, :], qkb[:, mi, :], ident_b[:]), reads=[Rqk, RC], writes=[RH(B3, 0)], inc=(mi == 1))
                yield
                S.op("dve", lambda e: e.tensor_copy(out=ATm[:], in_=ph(B1, 0)), reads=[RH(B1, 0)], writes=[RAT], defer=True)
                S.op("dve", lambda e: e.tensor_tensor(out=TT[:], in0=ident_f[:].unsqueeze(1).to_broadcast([128, 2, 128]), in1=ph(B1, 0), op=ALU.subtract), reads=[RH(B1, 0), RC], writes=[RTT])
                S.op("act", lambda e: e.activation(out=qkT[d][:, m0:m0 + 2, :], in_=phb(B3, 0)[:, 0:2, :], func=AF.Copy), reads=[RH(B3, 0)], writes=[R("qkT%d" % d)])
                yield
                Pc, PTc, Rp, Rpt = Am, ATm, RA, RAT
                for lev in range(1, 7):
                    j = lev % 2
                    RPn, RPTn = R("Pn%d_%d" % (j, d)), R("PTn%d_%d" % (j, d))
                    for mi in range(2):
                        S.op("pe", lambda e, mi=mi, Pc=Pc, PTc=PTc: e.matmul(ph(B2, 0)[:, mi, :], PTc[:, mi, :].bitcast(F32R), Pc[:, mi, :].bitcast(F32R), start=True, stop=True), reads=[Rp, Rpt], writes=[RH(B2, 0)], inc=(mi == 1))
                    if lev < 6:
                        for mi in range(2):
                            S.op("pe", lambda e, mi=mi, Pc=Pc, PTc=PTc: e.matmul(ph(B3, 1)[:, mi, :], Pc[:, mi, :].bitcast(F32R), PTc[:, mi, :].bitcast(F32R), start=True, stop=True), reads=[Rp, Rpt], writes=[RH(B3, 1)], inc=(mi == 1))
                    yield
                    S.op("dve", lambda e, j=j: e.tensor_copy(out=Pn[j][:], in_=ph(B2, 0)), reads=[RH(B2, 0)], writes=[RPn])
                    if lev < 6:
                        S.op("act", lambda e, j=j: e.activation(out=PTn[j][:], in_=ph(B3, 1), func=AF.Copy), reads=[RH(B3, 1)], writes=[RPTn])
                    yield
                    for mi in range(2):
                        S.op("pe", lambda e, mi=mi, j=j: e.matmul(ph(B1, 0)[:, mi, :], Pn[j][:, mi, :].bitcast(F32R), TT[:, mi, :].bitcast(F32R), start=True, stop=True), reads=[RPn, RTT], writes=[RH(B1, 0)], inc=(mi == 1))
                    yield
                    if lev < 6:
                        S.op("dve", lambda e: e.tensor_tensor(out=TT[:], in0=TT[:], in1=ph(B1, 0), op=ALU.add), reads=[RTT, RH(B1, 0)], writes=[RTT])
                    else:
                        S.op("dve", lambda e: e.tensor_tensor(out=TTb[:], in0=TT[:], in1=ph(B1, 0), op=ALU.add), reads=[RTT, RH(B1, 0)], writes=[RTb])
                    yield
                    Pc, PTc, Rp, Rpt = Pn[j], PTn[j], RPn, RPTn
                for mi in range(2):
                    S.op("pe", lambda e, mi=mi: e.matmul(ph(B2, 0)[:, mi, :], TTb[:, mi, :], vbg[:, mi, :], start=True, stop=True), reads=[RTb, Rvb], writes=[RH(B2, 0)], inc=(mi == 1))
                for mi in range(2):
                    S.op("pe", lambda e, mi=mi: e.matmul(ph(B3, 1)[:, mi, :], kwg[:, mi, :], TTb[:, mi, :], start=True, stop=True), reads=[RTb, Rkw], writes=[RH(B3, 1)], inc=(mi == 1))
                yield
                S.op("act", lambda e: e.activation(out=u_[d][:, m0:m0 + 2, :], in_=ph(B2, 0), func=AF.Copy), reads=[RH(B2, 0)], writes=[R("u%d" % d)], defer=True)
                S.op("act", lambda e: e.activation(out=nwT[d][:, m0:m0 + 2, :], in_=ph(B3, 1), func=AF.Identity, scale=-1.0), reads=[RH(B3, 1)], writes=[R("nwT%d" % d)])
                yield

            for G in range(8):
                m0 = G * 2
                hb7 = G % 2
                trv = phb(7, hb7)
                for mi in range(2):
                    S.op("pe", lambda e, mi=mi, trv=trv, m0=m0: e.transpose(trv[:, mi, :], kT[:, (m0 + mi) * 128:(m0 + mi + 1) * 128], ident_b[:]), reads=[RkT, RC], writes=[RH(7, hb7)], inc=False)
                for mi in range(2):
                    S.op("pe", lambda e, mi=mi, trv=trv, m0=m0: e.transpose(trv[:, 2 + mi, :], vT[:, (m0 + mi) * 128:(m0 + mi + 1) * 128], ident_b[:]), reads=[RvT, RC], writes=[RH(7, hb7)], inc=(mi == 1))
                for mi in range(2):
                    sl = slice((m0 + mi) * 128, (m0 + mi + 1) * 128)
                    S.op("pe", lambda e, mi=mi, sl=sl: e.matmul(ph(0, 0)[:, mi, :], kT[:, sl], kT[:, sl], start=True, stop=True), reads=[RkT], writes=[RH(0, 0)], inc=(mi == 1))
                for mi in range(2):
                    sl = slice((m0 + mi) * 128, (m0 + mi + 1) * 128)
                    S.op("pe", lambda e, mi=mi, sl=sl: e.matmul(ph(0, 1)[:, mi, :], qT[:, sl], kT[:, sl], start=True, stop=True), reads=[RqT, RkT], writes=[RH(0, 1)], inc=(mi == 1))
                gens = [unit(0, G, m0, hb7), unit(1, G, m0, hb7)]
                while gens:
                    for g in list(gens):
                        try:
                            next(g)
                        except StopIteration:
                            gens.remove(g)
                yield
        def phase2(h):
            for d in range(2):
                S.op("pool", lambda e, d=d: e.memset(Sst[d][:], 0.0), writes=[R("S%d" % d)])
                S.op("pool", lambda e, d=d: e.memset(Sbf[d][:], 0.0), writes=[R("Sbf%d" % d)])
            for step in range(NT):
                for d in range(2):
                    m = step if d == 0 else NT - 1 - step
                    b0 = d * 3
                    RS, RSb, Rvn, Ro = R("S%d" % d), R("Sbf%d" % d), R("vnew%d" % d), R("o%d" % d)
                    Ru, Rnw, Rqk2 = R("u%d" % d), R("nwT%d" % d), R("qkT%d" % d)
                    S.op("pe", lambda e, d=d, m=m, b0=b0: e.matmul(PS[b0][:, 0:128], nwT[d][:, m, :], Sbf[d][:], start=True, stop=True), reads=[Rnw, RSb], writes=[RP[b0]])
                    S.op("dve", lambda e, d=d, m=m, b0=b0: e.tensor_tensor(out=vnew[d][:], in0=PS[b0][:, 0:128], in1=u_[d][:, m, :], op=ALU.add), reads=[RP[b0], Ru], writes=[Rvn])
                    S.op("pe", lambda e, d=d, m=m, b0=b0: e.matmul(PS[b0 + 1][:, 0:128], qgT[d][:, m * 128:(m + 1) * 128], Sbf[d][:], start=True, stop=False), reads=[Rqg[d], RSb], writes=[RP[b0 + 1]], inc=False)
                    S.op("pe", lambda e, d=d, m=m, b0=b0: e.matmul(PS[b0 + 1][:, 0:128], qkT[d][:, m, :], vnew[d][:], start=False, stop=True), reads=[Rqk2, Rvn], writes=[RP[b0 + 1]])
                    S.op("act", lambda e, d=d, m=m, b0=b0: e.activation(out=o_d[d][:, m, :], in_=PS[b0 + 1][:, 0:128], func=AF.Copy), reads=[RP[b0 + 1]], writes=[Ro])
                    if step < NT - 1:
                        S.op("pe", lambda e, d=d, m=m, b0=b0: e.matmul(PS[b0 + 2][:, 0:128], kdm[d][:, m, :], vnew[d][:], start=True, stop=True), reads=[Rkdm, Rvn], writes=[RP[b0 + 2]])
                        S.op("dve", lambda e, d=d, m=m, b0=b0, h=h: e.scalar_tensor_tensor(out=Sst[d][:], in0=Sst[d][:], scalar=egl_t[:, d, m, h:h + 1], in1=PS[b0 + 2][:, 0:128], op0=ALU.mult, op1=ALU.add), reads=[RS, RT, RP[b0 + 2]], writes=[RS])
                        S.op("act", lambda e, d=d: e.activation(out=Sbf[d][:], in_=Sst[d][:], func=AF.Copy), reads=[RS], writes=[RSb])
                    yield
        def output(h):
            Ro0, Ro1 = R("o0"), R("o1")
            S.op("pool", lambda e: e.tensor_tensor(out=o_d[0][:], in0=o_d[0][:], in1=o_d[1][:], op=ALU.add), reads=[Ro0, Ro1], writes=[Ro0])
            yield
            S.op("pool", lambda e: e.tensor_tensor(out=o_d[1][:], in0=o_d[0][:], in1=o_d[0][:], op=ALU.mult), reads=[Ro0, Ro1], writes=[Ro1])
            yield
            S.op("dve", lambda e: e.tensor_reduce(out=ssn[:], in_=o_d[1][:], axis=AX.X, op=ALU.add), reads=[Ro1], writes=[R("ssn")])
            S.op("act", lambda e: e.activation(out=ssn[:], in_=ssn[:], func=AF.Sqrt, bias=eps_t[:], scale=1.0 / 128.0), reads=[R("ssn"), RC], writes=[R("ssn")])
            S.op("dve", lambda e: e.reciprocal(out=ssn[:], in_=ssn[:]), reads=[R("ssn")], writes=[R("ssn")])
            yield
            on = sqb[:].rearrange("p (m c) -> p m c", c=128)
            S.op("dve", lambda e: e.tensor_tensor(out=on, in0=o_d[0][:], in1=ssn[:].unsqueeze(2).to_broadcast([128, NT, 128]), op=ALU.mult), reads=[Ro0, R("ssn")], writes=[Rsq])
            yield
            for hb in range(2):
                for mi in range(8):
                    S.op("pe", lambda e, hb=hb, mi=mi: e.transpose(psb(6 + hb)[:, mi, :], on[:, hb * 8 + mi, :], ident_b[:]), reads=[Rsq, RC], writes=[RP[6 + hb]], inc=(mi == 7))
                S.op("dve", lambda e, hb=hb: e.scalar_tensor_tensor(out=sgate[:, hb * 1024:(hb + 1) * 1024], in0=psb(6 + hb).rearrange("p a b -> p (a b)"), scalar=gn_col[:, 0:1], in1=sgate[:, hb * 1024:(hb + 1) * 1024], op0=ALU.mult, op1=ALU.mult),
                     reads=[RP[6 + hb], RPAR, Rsg_], writes=[Rsg_])
                yield
            r0 = 1536 + h * 128
            S.dma("sp", lambda e, r0=r0: e.dma_start(out=yT[r0:r0 + 128, :], in_=sgate[:]), reads=[Rsg_], writes=[R("yT")])
        def drain(g):
            for _ in g:
                pass

        def interleave(ga, gb, ratio):
            a_live, b_live = True, gb is not None
            while a_live or b_live:
                if a_live:
                    try:
                        next(ga)
                    except StopIteration:
                        a_live = False
                if b_live:
                    for _ in range(ratio):
                        try:
                            next(gb)
                        except StopIteration:
                            b_live = False
                            break

        hl = list(heads)

        def chain(*gs):
            for g in gs:
                if g is not None:
                    yield from g

        drain(prep1(hl[0]))
        prev = None
        for idx, h in enumerate(hl):
            if "Bstop2" in phases:
                drain(prep2(h))
                break
            if DBG_SERIAL:
                drain(chain(output(prev) if prev is not None else None, prep2(h)))
                drain(phase1(h))
            else:
                interleave(phase1(h), chain(output(prev) if prev is not None else None, prep2(h)), 3)
            if "Bstop3" in phases:
                break
            nxt = prep1(hl[idx + 1]) if idx + 1 < len(hl) else None
            interleave(phase2(h), nxt, 2)
            prev = h
        if prev is not None and "Bstop2" not in phases and "Bstop3" not in phases:
            drain(output(prev))
        S.barrier()
        M.release(mk)

    BR = {}
    BR['B'] = branch_b
    BR['KV'] = branch_kvc
    BR['A'] = branch_a
    exec_hooks = {}

    with nc.Block() as block:
        for l in layers:
            load_layer_params(l)
            for s in range(NSEQ):
                xsrc = x_in[s] if l == 0 else x1[s]
                xdst = x1[s] if l == 0 else out[s]
                if "P2" in phases:
                    in_proj(l, xsrc)
                for name in ("KV", "TAB", "A", "B", "C"):
                    if name in phases and name in BR:
                        BR[name](l, s)
                if "OUT" in phases:
                    out_proj(l, xsrc, xdst)
        if "FIN" in phases:
            for s in range(NSEQ):
                final_norm(s)
        S.emit(block)
    nc._sched_stats = (S.n_ops, M.peak)
    return nc


_NC_CACHE = {}


def kernel(**inputs):
    xp = np.asarray(inputs["x_prompt"], np.float32)
    xs = np.asarray(inputs["x_sample"], np.float32)
    mp = np.asarray(inputs["mem_prompt"], np.float32)
    ms = np.asarray(inputs["mem_sample"], np.float32)
    seqs = [("p", i) for i in range(4)] + [("s", i) for i in range(8)]

    def get(kind, i):
        return (xp[i], mp[i]) if kind == "p" else (xs[i], ms[i])

    slots = [[seqs[c], seqs[8 + c % 4]] for c in range(8)]
    wnames = ["norm_g", "w_in", "sgu_ln_g", "sgu_ln_b", "sgu_w", "sgu_b", "conv_w", "a_log", "dt_bias", "gdn_norm_g", "mem_norm_g", "w_mem_kv", "w_out", "final_g"]
    wts = {k: np.ascontiguousarray(np.asarray(inputs[k], np.float32)) for k in wnames}
    in_maps = []
    for c in range(8):
        xa = np.stack([get(*slots[c][j])[0] for j in range(2)])
        ma = np.stack([get(*slots[c][j])[1] for j in range(2)])
        m = {"x": xa, "mem": ma}
        m.update(wts)
        in_maps.append(m)
    if "nc" not in _NC_CACHE:
        _NC_CACHE["nc"] = build_program()
    res = run_bass_kernel_spmd(_NC_CACHE["nc"], in_maps, core_ids=list(range(8)))
    yp = np.empty_like(xp)
    ys = np.empty_like(xs)
    for c in range(8):
        o = res.results[c]["out"]
        for j in range(2):
            if j == 1 and c >= 4:
                continue
            kind, i = slots[c][j]
            if kind == "p":
                yp[i] = o[j]
            else:
                ys[i] = o[j]
    return (yp, ys)
```
